# Optimizing a Trainium2 kernel written in Bass

```python
import math, functools
import jax, jax.numpy as jnp
from jax import lax
import numpy as np

D_MODEL = 2048
BATCH = 4
SEQ = 8192
DEPTH = 1
DEC_BATCH = 32
DEC_SEQ = 64
PAST_LEN = 2048

CHUNK = 64
N_META = 16
Q_BLOCK = 128
HEAD_DIM = 128
A_HEADS = 4
A_VDIM = 2 * HEAD_DIM
B_HEADS = 8
ROPE_DIM = HEAD_DIM // 4
ROPE_THETA = 500000.0
D_FF = 5632
EPS = 1e-6
FORGET_BIAS = 2.0
NEG = -1e30
A_Q = A_HEADS * 2 * HEAD_DIM
A_V = A_HEADS * A_VDIM
B_QK = B_HEADS * HEAD_DIM
N_IN = 2 * A_Q + A_V + 3 * B_QK + B_HEADS
SPLITS = (A_Q, 2 * A_Q, 2 * A_Q + A_V, 2 * A_Q + A_V + B_QK,
          2 * A_Q + A_V + 2 * B_QK, 2 * A_Q + A_V + 3 * B_QK)

kernel_name = 'hybrid_diff_fox_macaron_stream'


def _rmsnorm(x, g):
    xf = x.astype(jnp.float32)
    y = xf * lax.rsqrt(jnp.mean(xf * xf, axis=-1, keepdims=True) + EPS)
    return (y * g.astype(jnp.float32)).astype(x.dtype)


def _swiglu(h, w1, w3, w2):
    return (jax.nn.silu(h @ w1) * (h @ w3)) @ w2


def _rope(x, pos):
    half = ROPE_DIM // 2
    inv = jnp.power(ROPE_THETA, -jnp.arange(half, dtype=jnp.float32) * 2.0 / ROPE_DIM)
    ang = pos.astype(jnp.float32)[:, None] * inv[None, :]
    shape = (1, ang.shape[0]) + (1,) * (x.ndim - 3) + (half,)
    cos = jnp.cos(ang).reshape(shape)
    sin = jnp.sin(ang).reshape(shape)
    xr = x[..., :ROPE_DIM].astype(jnp.float32)
    x1, x2 = xr[..., :half], xr[..., half:]
    rot = jnp.concatenate([x1 * cos - x2 * sin, x2 * cos + x1 * sin], axis=-1)
    return jnp.concatenate([rot.astype(x.dtype), x[..., ROPE_DIM:]], axis=-1)


def _chunk_id(r):
    return jnp.where(r < N_META, -1, (r - N_META) // CHUNK)


def _project(h, pos, w_in, b_f, g_qa, g_ka, g_qb, g_kb):
    bsz, t = h.shape[:2]
    qa, ka, va, qb, kb, vb, fb = jnp.split(h @ w_in, SPLITS, axis=-1)
    qa = _rope(_rmsnorm(qa.reshape(bsz, t, A_HEADS, 2, HEAD_DIM), g_qa), pos)
    ka = _rope(_rmsnorm(ka.reshape(bsz, t, A_HEADS, 2, HEAD_DIM), g_ka), pos)
    va = va.reshape(bsz, t, A_HEADS, A_VDIM)
    qb = _rmsnorm(qb.reshape(bsz, t, B_HEADS, HEAD_DIM), g_qb)
    kb = _rmsnorm(kb.reshape(bsz, t, B_HEADS, HEAD_DIM), g_kb)
    vb = vb.reshape(bsz, t, B_HEADS, HEAD_DIM)
    logf = jax.nn.log_sigmoid((fb + b_f).astype(jnp.float32))
    return qa, ka, va, qb, kb, vb, logf


def _diff_attn(qa, ka, va, mask, lam):
    s = jnp.einsum('bqhmd,bkhmd->bhmqk', qa, ka).astype(jnp.float32) * (HEAD_DIM ** -0.5)
    p = jax.nn.softmax(jnp.where(mask, s, NEG), axis=-1)
    p = p[:, :, 0] - lam * p[:, :, 1]
    return jnp.einsum('bhqk,bkhe->bqhe', p.astype(va.dtype), va)


def _fox_attn(qb, kb, vb, cq, ck, mask):
    s = jnp.einsum('bqhd,bkhd->bhqk', qb, kb).astype(jnp.float32) * (HEAD_DIM ** -0.5)
    s = s + (jnp.transpose(cq, (0, 2, 1))[..., :, None] - jnp.transpose(ck, (0, 2, 1))[..., None, :])
    p = jax.nn.softmax(jnp.where(mask, s, NEG), axis=-1)
    return jnp.einsum('bhqk,bkhe->bqhe', p.astype(vb.dtype), vb)


def _mix_prompt(h, lam, w_in, b_f, g_qa, g_ka, g_qb, g_kb):
    bsz, L = h.shape[:2]
    qa, ka, va, qb, kb, vb, logf = _project(h, jnp.arange(L), w_in, b_f, g_qa, g_ka, g_qb, g_kb)
    c = jnp.cumsum(logf, axis=1)
    n_blk = -(-L // Q_BLOCK)
    Lp = n_blk * Q_BLOCK

    def to_blocks(a):
        a = jnp.pad(a, [(0, 0), (0, Lp - L)] + [(0, 0)] * (a.ndim - 2))
        return jnp.moveaxis(a.reshape((bsz, n_blk, Q_BLOCK) + a.shape[2:]), 1, 0)

    krow = jnp.arange(L)
    kchunk = _chunk_id(krow)

    def block(args):
        qa_b, qb_b, cq_b, start = args
        qrow = start + jnp.arange(Q_BLOCK)
        mask_a = kchunk[None, :] <= _chunk_id(qrow)[:, None]
        mask_b = krow[None, :] <= qrow[:, None]
        return (_diff_attn(qa_b, ka, va, mask_a, lam),
                _fox_attn(qb_b, kb, vb, cq_b, c, mask_b))

    starts = jnp.arange(n_blk) * Q_BLOCK
    oa, ob = lax.map(block, (to_blocks(qa), to_blocks(qb), to_blocks(c), starts))

    def from_blocks(o):
        return jnp.moveaxis(o, 0, 1).reshape((bsz, Lp) + o.shape[3:])[:, :L]

    return from_blocks(oa), from_blocks(ob), (ka, va, kb, vb, logf)


def _mix_sample(h, ck_a, cv_a, ck_b, cv_b, clogf, lam, w_in, b_f, g_qa, g_ka, g_qb, g_kb):
    bsz, t = h.shape[:2]
    past = ck_a.shape[1]
    pos = past + jnp.arange(t)
    qa, ka, va, qb, kb, vb, logf = _project(h, pos, w_in, b_f, g_qa, g_ka, g_qb, g_kb)
    ka_all = jnp.concatenate([ck_a.astype(ka.dtype), ka], axis=1)
    va_all = jnp.concatenate([cv_a.astype(va.dtype), va], axis=1)
    kb_all = jnp.concatenate([ck_b.astype(kb.dtype), kb], axis=1)
    vb_all = jnp.concatenate([cv_b.astype(vb.dtype), vb], axis=1)
    c = jnp.cumsum(jnp.concatenate([clogf.astype(jnp.float32), logf], axis=1), axis=1)
    mask_a = jnp.ones((t, past + t), dtype=bool)
    mask_b = jnp.arange(past + t)[None, :] <= pos[:, None]
    oa = _diff_attn(qa, ka_all, va_all, mask_a, lam)
    ob = _fox_attn(qb, kb_all, vb_all, c[:, past:], c, mask_b)
    return oa, ob, (ka, va, kb, vb, logf)


def _merge(oa, ob, g_oa, g_ob, w_out, lam_init):
    bsz, t = oa.shape[:2]
    oa = _rmsnorm(oa, g_oa) * (1.0 - lam_init)
    ob = _rmsnorm(ob, g_ob)
    return jnp.concatenate([oa.reshape(bsz, t, A_V), ob.reshape(bsz, t, B_QK)], axis=-1) @ w_out


def _layer(x, mixer, lam_init, g_ffn1, f1w1, f1w3, f1w2, g_mix, g_oa, g_ob, w_out,
           g_ffn2, f2w1, f2w3, f2w2, g_final):
    x = x + 0.5 * _swiglu(_rmsnorm(x, g_ffn1), f1w1, f1w3, f1w2)
    oa, ob, new = mixer(_rmsnorm(x, g_mix))
    x = x + _merge(oa, ob, g_oa, g_ob, w_out, lam_init)
    x = x + 0.5 * _swiglu(_rmsnorm(x, g_ffn2), f2w1, f2w3, f2w2)
    return _rmsnorm(x, g_final), new


def setup_inputs(seed: int = 0) -> dict:
    key = jax.random.key(seed)
    ks = iter(jax.random.split(key, 48))

    def nrm(shape, scale):
        return scale * jax.random.normal(next(ks), shape, jnp.float32)

    def gain(shape):
        return 1.0 + nrm(shape, 0.05)

    d = D_MODEL
    return {
        'x_prompt': nrm((BATCH, SEQ, d), 1.0),
        'x_sample': nrm((DEC_BATCH, DEC_SEQ, d), 1.0),
        'cache_a_k': nrm((DEPTH, DEC_BATCH, PAST_LEN, A_HEADS, 2, HEAD_DIM), 1.0),
        'cache_a_v': nrm((DEPTH, DEC_BATCH, PAST_LEN, A_HEADS, A_VDIM), 1.0),
        'cache_b_k': nrm((DEPTH, DEC_BATCH, PAST_LEN, B_HEADS, HEAD_DIM), 1.0),
        'cache_b_v': nrm((DEPTH, DEC_BATCH, PAST_LEN, B_HEADS, HEAD_DIM), 1.0),
        'cache_b_logf': jax.nn.log_sigmoid(FORGET_BIAS + nrm((DEPTH, DEC_BATCH, PAST_LEN, B_HEADS), 1.0)),
        'meta_tokens': nrm((N_META, d), 1.0),
        'g_ffn1': gain((DEPTH, d)),
        'ffn1_w1': nrm((DEPTH, d, D_FF), d ** -0.5),
        'ffn1_w3': nrm((DEPTH, d, D_FF), d ** -0.5),
        'ffn1_w2': nrm((DEPTH, D_FF, d), D_FF ** -0.5),
        'g_mix': gain((DEPTH, d)),
        'w_in': nrm((DEPTH, d, N_IN), d ** -0.5),
        'b_f': FORGET_BIAS + nrm((DEPTH, B_HEADS), 0.1),
        'g_qa': gain((DEPTH, HEAD_DIM)),
        'g_ka': gain((DEPTH, HEAD_DIM)),
        'g_qb': gain((DEPTH, HEAD_DIM)),
        'g_kb': gain((DEPTH, HEAD_DIM)),
        'lambda_q1': nrm((DEPTH, HEAD_DIM), 0.1),
        'lambda_k1': nrm((DEPTH, HEAD_DIM), 0.1),
        'lambda_q2': nrm((DEPTH, HEAD_DIM), 0.1),
        'lambda_k2': nrm((DEPTH, HEAD_DIM), 0.1),
        'g_oa': gain((DEPTH, A_VDIM)),
        'g_ob': gain((DEPTH, HEAD_DIM)),
        'w_out': nrm((DEPTH, A_V + B_QK, d), (A_V + B_QK) ** -0.5),
        'g_ffn2': gain((DEPTH, d)),
        'ffn2_w1': nrm((DEPTH, d, D_FF), d ** -0.5),
        'ffn2_w3': nrm((DEPTH, d, D_FF), d ** -0.5),
        'ffn2_w2': nrm((DEPTH, D_FF, d), D_FF ** -0.5),
        'g_final': gain((DEPTH, d)),
    }


def reference(x_prompt, x_sample, cache_a_k, cache_a_v, cache_b_k, cache_b_v, cache_b_logf,
              meta_tokens, g_ffn1, ffn1_w1, ffn1_w3, ffn1_w2, g_mix, w_in, b_f,
              g_qa, g_ka, g_qb, g_kb, lambda_q1, lambda_k1, lambda_q2, lambda_k2,
              g_oa, g_ob, w_out, g_ffn2, ffn2_w1, ffn2_w3, ffn2_w2, g_final):
    bsz = x_prompt.shape[0]
    meta = jnp.broadcast_to(meta_tokens.astype(x_prompt.dtype)[None], (bsz, N_META, x_prompt.shape[2]))
    xp = jnp.concatenate([meta, x_prompt], axis=1)
    xs = x_sample
    new_p = [[] for _ in range(5)]
    new_s = [[] for _ in range(5)]
    for l in range(DEPTH):
        lam_init = 0.8 - 0.6 * math.exp(-0.3 * l)
        lam = (jnp.exp(jnp.sum(lambda_q1[l].astype(jnp.float32) * lambda_k1[l].astype(jnp.float32)))
               - jnp.exp(jnp.sum(lambda_q2[l].astype(jnp.float32) * lambda_k2[l].astype(jnp.float32)))
               + lam_init)
        proj_w = (w_in[l], b_f[l], g_qa[l], g_ka[l], g_qb[l], g_kb[l])
        rest = (lam_init, g_ffn1[l], ffn1_w1[l], ffn1_w3[l], ffn1_w2[l], g_mix[l], g_oa[l], g_ob[l],
                w_out[l], g_ffn2[l], ffn2_w1[l], ffn2_w3[l], ffn2_w2[l], g_final[l])
        mix_p = functools.partial(_mix_prompt, lam=lam, w_in=proj_w[0], b_f=proj_w[1], g_qa=proj_w[2],
                                  g_ka=proj_w[3], g_qb=proj_w[4], g_kb=proj_w[5])
        mix_s = functools.partial(_mix_sample, ck_a=cache_a_k[l], cv_a=cache_a_v[l], ck_b=cache_b_k[l],
                                  cv_b=cache_b_v[l], clogf=cache_b_logf[l], lam=lam, w_in=proj_w[0],
                                  b_f=proj_w[1], g_qa=proj_w[2], g_ka=proj_w[3], g_qb=proj_w[4],
                                  g_kb=proj_w[5])
        xp, st_p = _layer(xp, mix_p, *rest)
        xs, st_s = _layer(xs, mix_s, *rest)
        for i in range(5):
            new_p[i].append(st_p[i])
            new_s[i].append(st_s[i])
    y_prompt = xp[:, N_META:]
    y_sample = xs
    return (y_prompt, y_sample,
            jnp.stack(new_p[0]), jnp.stack(new_p[1]), jnp.stack(new_p[2]), jnp.stack(new_p[3]), jnp.stack(new_p[4]),
            jnp.stack(new_s[0]), jnp.stack(new_s[1]), jnp.stack(new_s[2]), jnp.stack(new_s[3]), jnp.stack(new_s[4]))
```

```python
import contextlib
import os
import numpy as np
import concourse.bass as bass
import concourse.mybir as mybir
from concourse.bass_utils import run_bass_kernel_spmd

F32 = mybir.dt.float32
BF16 = mybir.dt.bfloat16
AF = mybir.ActivationFunctionType
ALU = mybir.AluOpType
AX = mybir.AxisListType
EPS = 1e-6
NEGM = -30000.0
SCALE = 128 ** -0.5
N_IN = 6152
LAM_INIT = 0.2
HV_OFF = dict(gqa=0, gka=128, gqb=256, gkb=384, goa=512, gob=768, bf=896, lq1=904, lk1=1032, lq2=1160, lk2=1288)
HV_LEN = 1416
MASK_COLS = 4176


class Cfg:
    def __init__(s, D=2048, DFF=5632, SEQ=8192, PAST=2048):
        s.D, s.DFF, s.SEQ, s.PAST = D, DFF, SEQ, PAST
        s.DC = D // 128
        s.FC = DFF // 128
        s.NFG = DFF // 256
        s.NBLK = SEQ // 512
        s.NPAIR = s.NBLK // 2
        s.NROWS = SEQ + 384
        s.NQ = s.NPAIR * 512 + 384
        s.NKT = SEQ // 128
        s.PKT = PAST // 128


class Buf:
    __slots__ = ("w", "r")

    def __init__(s):
        s.w = None
        s.r = {}


class DSem:
    def __init__(s, sem):
        s.sem = sem
        s.count = 0


class FW:
    def __init__(s, nc, es):
        s.nc = nc
        s.es = es
        s.eng = {"pe": nc.tensor, "act": nc.scalar, "dve": nc.vector, "pool": nc.gpsimd, "sp": nc.sync}
        s.sem = {k: es.enter_context(nc.semaphore("sem_" + k)) for k in s.eng}
        s.cnt = {k: 0 for k in s.eng}
        s.waited = {k: {} for k in s.eng}
        s.dsems = []

    def dsem(s, name):
        d = DSem(s.es.enter_context(s.nc.semaphore(name)))
        s.dsems.append(d)
        return d

    def wait(s, e, tk):
        if tk is None:
            return
        sem, val = tk
        k = id(sem)
        if s.waited[e].get(k, 0) >= val:
            return
        s.eng[e].wait_ge(sem, val)
        s.waited[e][k] = val

    def deps(s, e, reads, writes):
        own = s.sem[e]
        for b in reads:
            s.wait(e, b.w)
        for b in writes:
            if b.w is not None and b.w[0] is not own:
                s.wait(e, b.w)
            for tk in b.r.values():
                if tk[0] is not own:
                    s.wait(e, tk)

    def done(s, tk, reads, writes):
        k = id(tk[0])
        for b in reads:
            o = b.r.get(k)
            if o is None or o[1] < tk[1]:
                b.r[k] = tk
        for b in writes:
            b.w = tk
            b.r = {}

    def op(s, e, fn, reads=(), writes=()):
        s.deps(e, reads, writes)
        ins = fn(s.eng[e])
        s.cnt[e] += 1
        ins.then_inc(s.sem[e], 1)
        tk = (s.sem[e], s.cnt[e])
        s.done(tk, reads, writes)
        return tk

    def dma(s, e, ds, out, in_, reads=(), writes=()):
        s.deps(e, reads, writes)
        s.eng[e].dma_start(out=out, in_=in_).then_inc(ds.sem, 16)
        ds.count += 16
        tk = (ds.sem, ds.count)
        s.done(tk, reads, writes)
        return tk

    def barrier(s):
        for e in s.eng:
            for e2 in s.eng:
                if s.cnt[e2] > 0:
                    s.wait(e, (s.sem[e2], s.cnt[e2]))
            for d in s.dsems:
                if d.count > 0:
                    s.wait(e, (d.sem, d.count))


def build(cfg, phases=3):
    D, DFF, SEQ, PAST = cfg.D, cfg.DFF, cfg.SEQ, cfg.PAST
    DC, FC, NFG, NBLK, NPAIR = cfg.DC, cfg.FC, cfg.NFG, cfg.NBLK, cfg.NPAIR
    NROWS, NQ, NKT, PKT = cfg.NROWS, cfg.NQ, cfg.NKT, cfg.PKT
    nc = bass.Bass("TRN2", target_bir_lowering=False)

    def din(name, shape):
        return nc.dram_tensor(name, list(shape), F32, kind="ExternalInput").ap()

    def dout(name, shape):
        return nc.dram_tensor(name, list(shape), F32, kind="ExternalOutput").ap()

    def dscr(name, shape, dt):
        return nc.dram_tensor(name, list(shape), dt).ap()

    xin = din("xin", [NROWS, D])
    rope_in = din("rope", [NROWS, 32])
    ident_in = din("ident", [128, 128])
    umat_in = din("umat", [128, 128])
    lmat_in = din("lmat", [128, 128])
    sel_in = din("sel", [128, 384])
    masks_in = din("masks", [128, MASK_COLS])
    pm_in = din("pm", [128, 4])
    gains_in = din("gains", [128, 4 * DC])
    hvec_in = din("hvec", [1, HV_LEN])
    ck_in = din("ck", [4, PAST, 2048])
    cv_in = din("cv", [4, PAST, 2048])
    clf_in = din("clf", [4, PAST, 8])
    w1a = din("w1a", [D, DFF]); w3a = din("w3a", [D, DFF]); w2a = din("w2a", [DFF, D])
    w1b = din("w1b", [D, DFF]); w3b = din("w3b", [D, DFF]); w2b = din("w2b", [DFF, D])
    win = din("win", [D, N_IN]); wout = din("wout", [2048, D])

    y_out = dout("y_out", [NQ, D])
    ak_out = dout("ak_out", [NQ, 1024]); av_out = dout("av_out", [NQ, 1024])
    bk_out = dout("bk_out", [NQ, 1024]); bv_out = dout("bv_out", [NQ, 1024])
    lf_out = dout("lf_out", [NQ, 8])

    w1a_s = dscr("w1a_s", [NFG, 128, DC, 256], BF16); w3a_s = dscr("w3a_s", [NFG, 128, DC, 256], BF16)
    w1b_s = dscr("w1b_s", [NFG, 128, DC, 256], BF16); w3b_s = dscr("w3b_s", [NFG, 128, DC, 256], BF16)
    w2a_s = dscr("w2a_s", [DC, 128, FC, 128], BF16); w2b_s = dscr("w2b_s", [DC, 128, FC, 128], BF16)
    win_s = dscr("win_s", [24, 128, DC, 256], BF16)
    wfb_s = dscr("wfb_s", [128, DC, 8], BF16)
    wout_s = dscr("wout_s", [DC, 128, 16, 128], BF16)
    kT_s = dscr("kT_s", [16, 128, NROWS], BF16)
    qT_s = dscr("qT_s", [16, 128, NQ], BF16)
    v_s = dscr("v_s", [NROWS, 2048], BF16)
    oT_s = dscr("oT_s", [16, 128, NQ], BF16)
    x1_s = dscr("x1_s", [NPAIR + 1, 128, DC, 512], F32)

    es = contextlib.ExitStack()
    with es:
        fw = FW(nc, es)

        def sbg(name, shape, dt):
            return es.enter_context(nc.sbuf_tensor("g_" + name, list(shape), dt))

        banks = [es.enter_context(nc.psum_tensor(f"bank{i}", [128, 512], F32)) for i in range(8)]
        PB = [Buf() for _ in range(8)]

        ident_f = sbg("ident_f", [128, 128], F32); ident_b = sbg("ident_b", [128, 128], BF16)
        ones_f = sbg("ones_f", [128, 128], F32); ones_b = sbg("ones_b", [128, 128], BF16)
        umat = sbg("umat", [128, 128], F32); lmat = sbg("lmat", [128, 128], F32)
        masks_b = sbg("masks_b", [128, MASK_COLS], BF16)
        pm = sbg("pm", [128, 4], F32)
        gains = sbg("gains", [128, 4 * DC], F32)
        hv = sbg("hv", [128, HV_LEN], F32)
        lamt = sbg("lamt", [128, 8], F32)
        lamtmp = sbg("lamtmp", [128, 128], F32)
        lf_all = sbg("lf_all", [128, NKT + 3, 8], F32)
        BC = Buf()
        Blf = Buf()
        cs = fw.dsem("cs")
        fw.dma("sp", cs, ident_f[:], ident_in, writes=[BC])
        fw.dma("sp", cs, umat[:], umat_in, writes=[BC])
        fw.dma("sp", cs, lmat[:], lmat_in, writes=[BC])
        selm = sbg("selm", [128, 384], F32)
        fw.dma("sp", cs, selm[:], sel_in, writes=[BC])
        fw.dma("sp", cs, pm[:], pm_in, writes=[BC])
        fw.dma("sp", cs, gains[:], gains_in, writes=[BC])
        fw.dma("sp", cs, hv[:], hvec_in.partition_broadcast(128), writes=[BC])
        mcs = fw.dsem("mcs")
        fw.dma("pool", mcs, masks_b[:], masks_in, writes=[BC])
        fw.op("dve", lambda e: e.tensor_copy(out=ident_b[:], in_=ident_f[:]), reads=[BC], writes=[BC])
        fw.op("dve", lambda e: e.memset(ones_f[:], 1.0), writes=[BC])
        fw.op("dve", lambda e: e.memset(ones_b[:], 1.0), writes=[BC])
        zb = sbg("zb", [1, 512], BF16)
        fw.op("dve", lambda e: e.memset(zb[:], 0.0), writes=[BC])
        fw.op("dve", lambda e: e.tensor_scalar(out=gains[:], in0=gains[:], scalar1=float(D) ** 0.5, scalar2=None,
                                               op0=ALU.mult), reads=[BC], writes=[BC])
        o = HV_OFF
        fw.op("dve", lambda e: e.tensor_scalar(out=hv[:, 0:512], in0=hv[:, 0:512], scalar1=128.0 ** 0.5, scalar2=None,
                                               op0=ALU.mult), reads=[BC], writes=[BC])
        fw.op("dve", lambda e: e.tensor_scalar(out=hv[:, 512:768], in0=hv[:, 512:768],
                                               scalar1=(256.0 ** 0.5) * (1.0 - LAM_INIT), scalar2=None,
                                               op0=ALU.mult), reads=[BC], writes=[BC])
        fw.op("dve", lambda e: e.tensor_scalar(out=hv[:, 768:896], in0=hv[:, 768:896], scalar1=128.0 ** 0.5,
                                               scalar2=None, op0=ALU.mult), reads=[BC], writes=[BC])
        fw.op("dve", lambda e: e.tensor_tensor(out=lamtmp[:], in0=hv[:, o["lq1"]:o["lq1"] + 128],
                                               in1=hv[:, o["lk1"]:o["lk1"] + 128], op=ALU.mult), reads=[BC], writes=[BC])
        fw.op("dve", lambda e: e.tensor_reduce(out=lamt[:, 2:3], in_=lamtmp[:], axis=AX.X, op=ALU.add), reads=[BC], writes=[BC])
        fw.op("dve", lambda e: e.tensor_tensor(out=lamtmp[:], in0=hv[:, o["lq2"]:o["lq2"] + 128],
                                               in1=hv[:, o["lk2"]:o["lk2"] + 128], op=ALU.mult), reads=[BC], writes=[BC])
        fw.op("dve", lambda e: e.tensor_reduce(out=lamt[:, 3:4], in_=lamtmp[:], axis=AX.X, op=ALU.add), reads=[BC], writes=[BC])
        fw.op("act", lambda e: e.activation(out=lamt[:, 2:4], in_=lamt[:, 2:4], func=AF.Exp), reads=[BC], writes=[BC])
        fw.op("dve", lambda e: e.tensor_tensor(out=lamt[:, 0:1], in0=lamt[:, 2:3], in1=lamt[:, 3:4], op=ALU.subtract),
              reads=[BC], writes=[BC])
        fw.op("dve", lambda e: e.tensor_scalar(out=lamt[:, 0:1], in0=lamt[:, 0:1], scalar1=LAM_INIT, scalar2=None,
                                               op0=ALU.add), reads=[BC], writes=[BC])
        fw.op("dve", lambda e: e.tensor_scalar(out=lamt[:, 1:2], in0=lamt[:, 0:1], scalar1=-1.0, scalar2=None,
                                               op0=ALU.mult), reads=[BC], writes=[BC])
        fw.op("dve", lambda e: e.memset(lf_all[:], 0.0), writes=[Blf])
        epsD = sbg("epsD", [128, 4], F32)
        fw.op("dve", lambda e: e.memset(epsD[:, 0:1], float(D) * EPS), writes=[BC])
        fw.op("dve", lambda e: e.memset(epsD[:, 1:2], 128.0 * EPS), writes=[BC])
        fw.op("dve", lambda e: e.memset(epsD[:, 2:3], 256.0 * EPS), writes=[BC])
        fw.op("dve", lambda e: e.memset(epsD[:, 3:4], 1.0), writes=[BC])

        cast_sems = [fw.dsem(f"cast{i}") for i in range(6)]
        cast_n = [0]

        def cast(out_ap, in_ap, buf):
            ds = cast_sems[cast_n[0] % 6]
            if ds.count > 0:
                fw.wait("pool", (ds.sem, ds.count))
            fw.dma("pool", ds, out_ap, in_ap, writes=[buf])
            cast_n[0] += 1

        def cast_up(ws, w):
            bufs = []
            wv = w.rearrange("(c p) (g j) -> g p c j", p=128, j=256)
            for g in range(NFG):
                b = Buf(); bufs.append(b)
                cast(ws[g], wv[g], b)
            return bufs

        def cast_down(ws, w, nfc):
            bufs = []
            wv = w.rearrange("(f p) (c j) -> c p f j", p=128, j=128)
            for c in range(DC):
                b = Buf(); bufs.append(b)
                h = (nfc + 1) // 2
                cast(ws[c][:, 0:h, :], wv[c][:, 0:h, :], b)
                if nfc > h:
                    b2 = Buf()
                    cast(ws[c][:, h:nfc, :], wv[c][:, h:nfc, :], b2)
                    bufs[-1] = (b, b2)
            return bufs

        Bw1a = []; Bw3a = []
        w1v = w1a.rearrange("(c p) (g j) -> g p c j", p=128, j=256)
        w3v = w3a.rearrange("(c p) (g j) -> g p c j", p=128, j=256)
        for g in range(NFG):
            b1 = Buf(); b3 = Buf(); Bw1a.append(b1); Bw3a.append(b3)
            cast(w1a_s[g], w1v[g], b1)
            cast(w3a_s[g], w3v[g], b3)
        Bw2a = cast_down(w2a_s, w2a, FC)
        Bwin = []
        for g in range(24):
            b = Buf(); Bwin.append(b)
            cast(win_s[g], win[:, g * 256:(g + 1) * 256].rearrange("(c p) j -> p c j", p=128), b)
        b = Buf(); Bwin.append(b)
        cast(wfb_s, win[:, 6144:6152].rearrange("(c p) j -> p c j", p=128), b)
        wfb = sbg("wfb", [128, DC, 8], BF16)
        Swfb = fw.dsem("Swfb")
        Bwfb = Buf()
        Bwfb_cast = b
        Bw1b = cast_up(w1b_s, w1b)
        Bw3b = cast_up(w3b_s, w3b)
        Bw2b = cast_down(w2b_s, w2b, FC)
        Bwout = cast_down(wout_s, wout, 16)

        class Ctx:
            pass

        def rmsnorm_fm(c, N, gcol0, final=False):
            for ch in range(DC):
                if ch % 2 == 0:
                    fw.op("act", lambda e, ch=ch: e.activation(out=c.hT[:, ch, 0:N], in_=c.xT[:, ch, 0:N], func=AF.Square),
                          reads=[c.BxT[ch]], writes=[c.BhT[ch]])
                else:
                    fw.op("dve", lambda e, ch=ch: e.tensor_tensor(out=c.hT[:, ch, 0:N], in0=c.xT[:, ch, 0:N],
                                                                  in1=c.xT[:, ch, 0:N], op=ALU.mult),
                          reads=[c.BxT[ch]], writes=[c.BhT[ch]])

            def f(e):
                ins = None
                for ch in range(DC):
                    ins = e.matmul(banks[6][:, 0:N], ones_b[:], c.hT[:, ch, 0:N], start=(ch == 0), stop=(ch == DC - 1))
                return ins
            fw.op("pe", f, reads=c.BhT + [BC], writes=[PB[6]])
            fw.op("act", lambda e: e.activation(out=c.rstd[:, 0:N], in_=banks[6][:, 0:N], func=AF.Sqrt,
                                                bias=epsD[:, 0:1], scale=1.0),
                  reads=[PB[6], BC], writes=[c.Brstd])
            fw.op("dve", lambda e: e.reciprocal(out=c.rstd[:, 0:N], in_=c.rstd[:, 0:N]),
                  reads=[c.Brstd], writes=[c.Brstd])
            for ch in range(DC):
                eng = "dve"
                if final:
                    fw.op(eng, lambda e, ch=ch: e.scalar_tensor_tensor(
                        out=c.xT[:, ch, 0:N], in0=c.xT[:, ch, 0:N], scalar=gains[:, gcol0 + ch:gcol0 + ch + 1],
                        in1=c.rstd[:, 0:N], op0=ALU.mult, op1=ALU.mult),
                        reads=[c.BxT[ch], c.Brstd, BC, c.BhT[ch]], writes=[c.BxT[ch]])
                else:
                    fw.op(eng, lambda e, ch=ch: e.scalar_tensor_tensor(
                        out=c.hT[:, ch, 0:N], in0=c.xT[:, ch, 0:N], scalar=gains[:, gcol0 + ch:gcol0 + ch + 1],
                        in1=c.rstd[:, 0:N], op0=ALU.mult, op1=ALU.mult),
                        reads=[c.BxT[ch], c.Brstd, BC], writes=[c.BhT[ch]])

        def ffn(c, N, w1s, w3s, w2s, Bw1, Bw3, Bw2):
            for g in range(NFG):
                sa = (g % 2) * 2
                fw.dma("sp", c.Swsl[sa], c.wsl[sa][:], w1s[g], reads=[Bw1[g]], writes=[c.Bwsl[sa]])
                fw.dma("sp", c.Swsl[sa + 1], c.wsl[sa + 1][:], w3s[g], reads=[Bw3[g]], writes=[c.Bwsl[sa + 1]])
                for j in range(2):
                    fc = 2 * g + j
                    bu = (fc % 2) * 2
                    for k, sl in ((0, sa), (1, sa + 1)):
                        def f(e, k=k, sl=sl, bu=bu, j=j):
                            ins = None
                            for ch in range(DC):
                                ins = e.matmul(banks[bu + k][:, 0:N], c.wsl[sl][:, ch, j * 128:(j + 1) * 128],
                                               c.hT[:, ch, 0:N], start=(ch == 0), stop=(ch == DC - 1))
                            return ins
                        fw.op("pe", f, reads=c.BhT + [c.Bwsl[sl]], writes=[PB[bu + k]])
                    sgi = fc % 2
                    fw.op("act", lambda e, bu=bu, sgi=sgi: e.activation(out=c.sg[sgi][:, 0:N], in_=banks[bu][:, 0:N],
                                                                        func=AF.Silu),
                          reads=[PB[bu]], writes=[c.Bsg[sgi]])
                    fw.op("dve", lambda e, bu=bu, sgi=sgi, fc=fc: e.tensor_tensor(
                        out=c.gT[:, fc, 0:N], in0=c.sg[sgi][:, 0:N], in1=banks[bu + 1][:, 0:N], op=ALU.mult),
                        reads=[c.Bsg[sgi], PB[bu + 1]], writes=[c.BgT[fc]])
            for dc in range(DC):
                sl = dc % 2
                fw.dma("sp", c.Sw2sl[sl], c.w2sl[sl][:], w2s[dc], reads=list(Bw2[dc]) if isinstance(Bw2[dc], tuple) else [Bw2[dc]], writes=[c.Bw2sl[sl]])

                def f(e, sl=sl, dc=dc):
                    ins = None
                    for fc in range(FC):
                        ins = e.matmul(banks[4 + dc % 2][:, 0:N], c.w2sl[sl][:, fc, :], c.gT[:, fc, 0:N],
                                       start=(fc == 0), stop=(fc == FC - 1))
                    return ins
                fw.op("pe", f, reads=c.BgT + [c.Bw2sl[sl]], writes=[PB[4 + dc % 2]])
                fw.op("dve", lambda e, dc=dc: e.scalar_tensor_tensor(
                    out=c.xT[:, dc, 0:N], in0=banks[4 + dc % 2][:, 0:N], scalar=0.5, in1=c.xT[:, dc, 0:N],
                    op0=ALU.mult, op1=ALU.add),
                    reads=[PB[4 + dc % 2], c.BxT[dc]], writes=[c.BxT[dc]])

        def alloc_common(ph, c):
            def sb(name, shape, dt):
                return ph.enter_context(nc.sbuf_tensor(c.tag + "_" + name, list(shape), dt))
            c.sb = sb
            c.xT = sb("xT", [128, DC, 512], F32); c.BxT = [Buf() for _ in range(DC)]
            c.hT = sb("hT", [128, DC, 512], BF16); c.BhT = [Buf() for _ in range(DC)]
            bigsz = max(FC * 512, 16384)
            c.big = sb("big", [128, bigsz], BF16)
            c.gT = c.big[:, 0:FC * 512].rearrange("p (f n) -> p f n", n=512)
            c.BgT = [Buf() for _ in range(FC)]
            c.wsl = [sb(f"wsl{i}", [128, DC, 256], BF16) for i in range(4)]
            c.Bwsl = [Buf() for _ in range(4)]; c.Swsl = [fw.dsem(f"{c.tag}wsl{i}") for i in range(4)]
            c.w2sl = [sb(f"w2sl{i}", [128, FC, 128], BF16) for i in range(2)]
            c.Bw2sl = [Buf() for _ in range(2)]; c.Sw2sl = [fw.dsem(f"{c.tag}w2sl{i}") for i in range(2)]
            c.sg = [sb(f"sg{i}", [128, 512], BF16) for i in range(2)]; c.Bsg = [Buf(), Buf()]
            c.rstd = sb("rstd", [128, 512], F32); c.Brstd = Buf()

        def phaseA():
            with contextlib.ExitStack() as ph:
                c = Ctx(); c.tag = "A"
                alloc_common(ph, c)
                sb = c.sb
                xs0 = sb("xs", [128, D], F32)
                stage32 = c.big[:, 0:16384].bitcast(F32)
                if FC * 512 >= 16384 + 2 * D:
                    xs1 = c.big[:, 16384:16384 + 2 * D].bitcast(F32)
                    xsl = [xs0[:], xs1]
                else:
                    xsl = [xs0[:], xs0[:]]
                nxs = 2 if FC * 512 >= 16384 + 2 * D else 1
                Bxsl = [Buf() for _ in range(nxs)]; Sxsl = [fw.dsem(f"Sxs{i}") for i in range(nxs)]

                def stage(par):
                    return stage32[:, par * 4096:(par + 1) * 4096].rearrange("p (t e) -> p t e", e=1024)
                Bstage = [Buf(), Buf()]
                ropet = sb("ropet", [128, 4, 32], F32); Bropet = Buf(); Sropet = fw.dsem("Sropet")
                junk = sb("junk", [128, 128], BF16); Bjunk = Buf()
                ssq = [sb(f"ssq{i}", [128, 2], F32) for i in range(6)]; Bssq = [Buf() for _ in range(6)]
                rs2 = [sb(f"rs2{i}", [128, 2], F32) for i in range(6)]; Brs2 = [Buf() for _ in range(6)]
                b16 = [sb(f"b16_{i}", [128, 1024], BF16) for i in range(2)]; Bb16 = [Buf(), Buf()]
                Tst = [sb(f"Tst{i}", [128, 8, 512], BF16) for i in range(2)]; BTst = [Buf(), Buf()]
                rt = [sb(f"rt{i}", [128, 8, 16], F32) for i in range(4)]; Brt = Buf()
                zt = sb("zt", [128, 8], F32); Bzt = Buf()
                Sst = [fw.dsem(f"Sst{i}") for i in range(6)]
                Sx1 = fw.dsem("Sx1"); Slf = fw.dsem("Slf")
                types = [("qa", 0, o["gqa"], True), ("ka", 4, o["gka"], True), ("va", 8, None, False),
                         ("qb", 12, o["gqb"], False), ("kb", 16, o["gkb"], False), ("vb", 20, None, False)]
                tcount = [0]
                gcount = [0]
                qcount = [0]
                pending = []

                for b in range(NBLK + 1):
                    misc = (b == NBLK)
                    nt = 3 if misc else 4
                    N = nt * 128
                    own = misc or (b % 2 == 0)
                    oi = NPAIR if misc else b // 2
                    row0 = b * 512
                    orow0 = oi * 512
                    for sbuf in Bstage:
                        for gb in c.BgT:
                            if sbuf.w is not None:
                                gb.r[("w", id(sbuf))] = sbuf.w
                            for k2, tk in sbuf.r.items():
                                gb.r[(k2, id(sbuf))] = tk
                    fw.dma("sp", Sropet, ropet[:, 0:nt, :],
                           rope_in[row0:row0 + N, :].rearrange("(t p) e -> p t e", p=128), writes=[Bropet])
                    for ts in range(nt):
                        xi = ts % nxs
                        xs = xsl[xi]; Bxs = Bxsl[xi]
                        fw.dma("sp", Sxsl[xi], xs, xin[row0 + ts * 128: row0 + (ts + 1) * 128, :],
                               writes=[Bxs] + (c.BgT if xi == 1 else []))
                        for q4 in range((DC + 3) // 4):
                            nch = min(4, DC - q4 * 4)

                            def f(e, q4=q4, nch=nch, xs=None):
                                ins = None
                                for k in range(nch):
                                    ch = q4 * 4 + k
                                    ins = e.transpose(banks[q4 % 4][:, k * 128:(k + 1) * 128],
                                                      xs[:, ch * 128:(ch + 1) * 128], ident_f[:])
                                return ins
                            fw.op("pe", lambda e, f=f, xs=xs: f(e, xs=xs), reads=[Bxs, BC], writes=[PB[q4 % 4]])
                            eng = "act" if q4 % 2 == 0 else "dve"
                            if eng == "act":
                                fn = lambda e, q4=q4, nch=nch, ts=ts: e.activation(
                                    out=c.xT[:, q4 * 4:q4 * 4 + nch, ts * 128:(ts + 1) * 128],
                                    in_=banks[q4 % 4][:, 0:nch * 128].rearrange("p (c n) -> p c n", n=128), func=AF.Copy)
                            else:
                                fn = lambda e, q4=q4, nch=nch, ts=ts: e.tensor_copy(
                                    out=c.xT[:, q4 * 4:q4 * 4 + nch, ts * 128:(ts + 1) * 128],
                                    in_=banks[q4 % 4][:, 0:nch * 128].rearrange("p (c n) -> p c n", n=128))
                            fw.op(eng, fn, reads=[PB[q4 % 4]], writes=c.BxT[q4 * 4:q4 * 4 + nch])
                    rmsnorm_fm(c, N, 0)
                    ffn(c, N, w1a_s, w3a_s, w2a_s, Bw1a, Bw3a, Bw2a)
                    if own:
                        fw.dma("act", Sx1, x1_s[oi][:, :, 0:N], c.xT[:, :, 0:N], reads=c.BxT)
                    rmsnorm_fm(c, N, DC)
                    for sbuf in Bstage:
                        for gb in c.BgT:
                            for k2, tk in gb.r.items():
                                sbuf.r[(k2, id(gb))] = tk
                            if gb.w is not None:
                                sbuf.r[("w", id(gb))] = gb.w
                    for (tname, g0, goff, dorope) in types:
                        isq = tname[0] == "q"
                        isv = tname[0] == "v"
                        if isq and not own:
                            continue
                        par = tcount[0] % 2
                        tcount[0] += 1
                        st = stage(par)
                        Bst = Bstage[par]
                        for gi in range(4):
                            g = g0 + gi
                            sl = gcount[0] % 4
                            gcount[0] += 1
                            fw.dma("sp", c.Swsl[sl], c.wsl[sl][:], win_s[g], reads=[Bwin[g]], writes=[c.Bwsl[sl]])
                            for ts in range(nt):
                                bk = qcount[0] % 6
                                qcount[0] += 1

                                def f(e, sl=sl, ts=ts, bk=bk):
                                    ins = None
                                    for ch in range(DC):
                                        ins = e.matmul(banks[bk][:, 0:256], c.hT[:, ch, ts * 128:(ts + 1) * 128],
                                                       c.wsl[sl][:, ch, :], start=(ch == 0), stop=(ch == DC - 1))
                                    return ins
                                fw.op("pe", f, reads=c.BhT + [c.Bwsl[sl]], writes=[PB[bk]])
                                if isv:
                                    fw.op("act", lambda e, ts=ts, gi=gi, bk=bk, st=st: e.activation(
                                        out=st[:, ts, gi * 256:(gi + 1) * 256], in_=banks[bk][:, 0:256], func=AF.Copy),
                                        reads=[PB[bk]], writes=[Bst])
                                else:
                                    for hm in range(2):
                                        fw.op("act", lambda e, hm=hm, bk=bk: e.activation(
                                            out=junk[:], in_=banks[bk][:, hm * 128:(hm + 1) * 128], func=AF.Square,
                                            accum_out=ssq[bk][:, hm:hm + 1]),
                                            reads=[PB[bk]], writes=[Bjunk, Bssq[bk]])
                                    fw.op("act", lambda e, bk=bk: e.activation(
                                        out=rs2[bk][:], in_=ssq[bk][:], func=AF.Sqrt, bias=epsD[:, 1:2], scale=1.0),
                                        reads=[Bssq[bk], BC], writes=[Brs2[bk]])
                                    fw.op("dve", lambda e, bk=bk: e.reciprocal(out=rs2[bk][:], in_=rs2[bk][:]),
                                          reads=[Brs2[bk]], writes=[Brs2[bk]])
                                    for hm in range(2):
                                        fw.op("dve", lambda e, hm=hm, bk=bk, ts=ts, gi=gi, st=st, goff=goff: e.scalar_tensor_tensor(
                                            out=st[:, ts, gi * 256 + hm * 128: gi * 256 + (hm + 1) * 128],
                                            in0=banks[bk][:, hm * 128:(hm + 1) * 128], scalar=rs2[bk][:, hm:hm + 1],
                                            in1=hv[:, goff:goff + 128], op0=ALU.mult, op1=ALU.mult),
                                            reads=[PB[bk], Brs2[bk], BC], writes=[Bst])
                        while pending:
                            pending.pop(0)()
                        tpar = par
                        for ts in range(nt):
                            if dorope:
                                S3 = st[:, ts, :].rearrange("p (h d) -> p h d", d=128)
                                x1 = S3[:, :, 0:16]; x2 = S3[:, :, 16:32]
                                cosb = ropet[:, ts, 0:16].unsqueeze(1).broadcast_to([128, 8, 16])
                                sinb = ropet[:, ts, 16:32].unsqueeze(1).broadcast_to([128, 8, 16])
                                rd = [Bst, Bropet]
                                fw.op("dve", lambda e, x1=x1, cosb=cosb: e.tensor_tensor(out=rt[0][:], in0=x1, in1=cosb, op=ALU.mult), reads=rd, writes=[Brt])
                                fw.op("dve", lambda e, x2=x2, sinb=sinb: e.tensor_tensor(out=rt[1][:], in0=x2, in1=sinb, op=ALU.mult), reads=rd, writes=[Brt])
                                fw.op("dve", lambda e, x1=x1, sinb=sinb: e.tensor_tensor(out=rt[2][:], in0=x1, in1=sinb, op=ALU.mult), reads=rd, writes=[Brt])
                                fw.op("dve", lambda e, x2=x2, cosb=cosb: e.tensor_tensor(out=rt[3][:], in0=x2, in1=cosb, op=ALU.mult), reads=rd, writes=[Brt])
                                fw.op("dve", lambda e, x1=x1: e.tensor_tensor(out=x1, in0=rt[0][:], in1=rt[1][:], op=ALU.subtract), reads=[Brt], writes=[Bst])
                                fw.op("dve", lambda e, x2=x2: e.tensor_tensor(out=x2, in0=rt[2][:], in1=rt[3][:], op=ALU.add), reads=[Brt], writes=[Bst])
                            bi = ts % 2
                            fw.op("dve", lambda e, bi=bi, ts=ts, st=st: e.tensor_copy(out=b16[bi][:], in_=st[:, ts, :]),
                                  reads=[Bst], writes=[Bb16[bi]])
                            if isv:
                                c0 = 0 if tname == "va" else 1024
                                fw.dma("act", Sst[2 + bi], v_s[row0 + ts * 128: row0 + (ts + 1) * 128, c0:c0 + 1024],
                                       b16[bi][:], reads=[Bb16[bi]])
                            else:
                                def f(e, bi=bi):
                                    ins = None
                                    for hm in range(8):
                                        ins = e.transpose(banks[6 + bi][:].bitcast(BF16)[:, hm * 128:(hm + 1) * 128],
                                                          b16[bi][:, hm * 128:(hm + 1) * 128], ident_b[:])
                                    return ins
                                fw.op("pe", f, reads=[Bb16[bi], BC], writes=[PB[6 + bi]])
                                fw.op("dve", lambda e, bi=bi, ts=ts, tpar=tpar: e.tensor_copy(
                                    out=Tst[tpar][:, :, ts * 128:(ts + 1) * 128],
                                    in_=banks[6 + bi][:].bitcast(BF16).rearrange("p (h n) -> p h n", n=128)),
                                    reads=[PB[6 + bi]], writes=[BTst[tpar]])
                        def mk_stores(tname=tname, isq=isq, isv=isv, own=own, par=par, tpar=tpar, st=st, Bst=Bst,
                                      orow0=orow0, row0=row0, N=N, nt=nt):
                            if (not isq) and own:
                                dst = {"ka": ak_out, "va": av_out, "kb": bk_out, "vb": bv_out}[tname]
                                fw.dma("act", Sst[par], dst[orow0:orow0 + N, :].rearrange("(t p) e -> p t e", p=128),
                                       st[:, 0:nt, :], reads=[Bst])
                            if not isv:
                                hm0 = 0 if tname[1] == "a" else 8
                                if isq:
                                    fw.dma("act", Sst[4 + tpar],
                                           qT_s[hm0:hm0 + 8, :, orow0:orow0 + N].rearrange("h d n -> d h n"),
                                           Tst[tpar][:, :, 0:N], reads=[BTst[tpar]])
                                else:
                                    fw.dma("act", Sst[4 + tpar],
                                           kT_s[hm0:hm0 + 8, :, row0:row0 + N].rearrange("h d n -> d h n"),
                                           Tst[tpar][:, :, 0:N], reads=[BTst[tpar]])
                        pending.append(mk_stores)
                    while pending:
                        pending.pop(0)()
                    if b == 0:
                        fw.dma("sp", Swfb, wfb[:], wfb_s, reads=[Bwfb_cast], writes=[Bwfb])
                    for ts in range(nt):
                        bk = qcount[0] % 6
                        qcount[0] += 1

                        def f(e, ts=ts, bk=bk):
                            ins = None
                            for ch in range(DC):
                                ins = e.matmul(banks[bk][:, 0:8], c.hT[:, ch, ts * 128:(ts + 1) * 128],
                                               wfb[:, ch, :], start=(ch == 0), stop=(ch == DC - 1))
                            return ins
                        fw.op("pe", f, reads=c.BhT + [Bwfb], writes=[PB[bk]])
                        tile_i = b * 4 + ts
                        fw.op("dve", lambda e, bk=bk: e.tensor_tensor(out=zt[:], in0=banks[bk][:, 0:8],
                                                                      in1=hv[:, o["bf"]:o["bf"] + 8], op=ALU.add),
                              reads=[PB[bk], BC], writes=[Bzt])
                        fw.op("act", lambda e: e.activation(out=zt[:], in_=zt[:], func=AF.Exp, scale=-1.0),
                              reads=[Bzt], writes=[Bzt])
                        fw.op("act", lambda e: e.activation(out=zt[:], in_=zt[:], func=AF.Ln, bias=epsD[0:128, 3:4], scale=1.0),
                              reads=[Bzt], writes=[Bzt])
                        fw.op("dve", lambda e, tile_i=tile_i: e.tensor_scalar(out=lf_all[:, tile_i, :], in0=zt[:], scalar1=-1.0,
                                                                              scalar2=None, op0=ALU.mult),
                              reads=[Bzt], writes=[Blf])
                    if own:
                        fw.dma("act", Slf, lf_out[orow0:orow0 + N, :].rearrange("(t p) h -> p t h", p=128),
                               lf_all[:, b * 4:b * 4 + nt, :], reads=[Blf])
                fw.barrier()

        def phaseB():
            with contextlib.ExitStack() as ph:
                def sb(name, shape, dt):
                    return ph.enter_context(nc.sbuf_tensor("B_" + name, list(shape), dt))
                MT = NKT + 2
                tot_all = sb("tot_all", [128, NKT, 8], F32); cum_all = sb("cum_all", [128, NKT, 8], F32)
                negc = sb("negc", [128, NKT + 1, 8], F32)
                cref = sb("cref", [128, NPAIR, 8], F32)
                carry = sb("carry", [128, 8], F32); mtot = sb("mtot", [128, 8], F32)
                tb0 = sb("tb0", [128, 8], F32); tb1 = sb("tb1", [128, 8], F32)
                rb = sb("rb", [128, 8], F32); metab = sb("metab", [128, 8], F32)
                clf_t = sb("clf_t", [128, 4, PKT, 8], F32); Sclf = fw.dsem("Sclf")
                sbias_c = sb("sbias_c", [128, 4 * PKT, 8], F32); sbias_n = sb("sbias_n", [128, 4, 8], F32)
                zcol = sb("zcol", [128, 1], F32)
                Bcs = Buf()
                fw.dma("sp", Sclf, clf_t[:], clf_in.rearrange("b (t p) h -> p b t h", p=128), writes=[Bcs])
                fw.op("dve", lambda e: e.memset(zcol[:], 0.0), writes=[Bcs])
                zpad = sb("zpad", [128, 16, 112], BF16); Bzpad = Buf(); Szpad = fw.dsem("Szpad")
                fw.op("dve", lambda e: e.memset(zpad[:], 0.0), writes=[Bzpad])
                fw.dma("act", Szpad, oT_s[:, :, NQ - 112:NQ].rearrange("c d n -> d c n"), zpad[:], reads=[Bzpad])
                fw.op("dve", lambda e: e.memset(negc[:], 0.0), writes=[Bcs])

                def f(e):
                    ins = None
                    for j in range(NKT):
                        ins = e.matmul(banks[0][:, j * 8:(j + 1) * 8], ones_f[:], lf_all[:, j, :], start=True, stop=True)
                    return ins
                fw.op("pe", f, reads=[Blf, BC], writes=[PB[0]])

                def f(e):
                    ins = None
                    for j in range(NKT):
                        ins = e.matmul(banks[1][:, j * 8:(j + 1) * 8], umat[:], lf_all[:, j, :], start=True, stop=True)
                    return ins
                fw.op("pe", f, reads=[Blf, BC], writes=[PB[1]])

                def f(e):
                    e.matmul(banks[2][:, 0:8], ones_f[0:16, :], lf_all[0:16, MT, :], start=True, stop=True)
                    return e.matmul(banks[2][0:16, 8:16], umat[0:16, 0:16], lf_all[0:16, MT, :], start=True, stop=True)
                fw.op("pe", f, reads=[Blf, BC], writes=[PB[2]])
                fw.op("dve", lambda e: e.tensor_copy(out=tot_all[:], in_=banks[0][:, 0:NKT * 8].rearrange("p (t h) -> p t h", h=8)),
                      reads=[PB[0]], writes=[Bcs])
                fw.op("dve", lambda e: e.tensor_copy(out=cum_all[:], in_=banks[1][:, 0:NKT * 8].rearrange("p (t h) -> p t h", h=8)),
                      reads=[PB[1]], writes=[Bcs])
                fw.op("dve", lambda e: e.tensor_copy(out=carry[:], in_=banks[2][:, 0:8]), reads=[PB[2]], writes=[Bcs])
                fw.op("dve", lambda e: e.tensor_copy(out=mtot[:], in_=banks[2][:, 0:8]), reads=[PB[2]], writes=[Bcs])
                fw.op("dve", lambda e: e.tensor_scalar(out=negc[0:16, NKT, :], in0=banks[2][0:16, 8:16], scalar1=-1.0, scalar2=None,
                                                       op0=ALU.mult), reads=[PB[2]], writes=[Bcs])
                fw.op("dve", lambda e: e.tensor_tensor(out=metab[0:16, :], in0=negc[0:16, NKT, :], in1=mtot[0:16, :], op=ALU.add),
                      reads=[Bcs], writes=[Bcs])

                def dv(fn):
                    fw.op("dve", fn, reads=[Bcs, BC], writes=[Bcs])
                for i in range(NPAIR):
                    t0 = 8 * i
                    dv(lambda e, t0=t0: e.tensor_tensor(out=tb0[:], in0=tot_all[:, t0, :], in1=tot_all[:, t0 + 1, :], op=ALU.add))
                    dv(lambda e, t0=t0: e.tensor_tensor(out=tb0[:], in0=tb0[:], in1=tot_all[:, t0 + 2, :], op=ALU.add))
                    dv(lambda e, t0=t0: e.tensor_tensor(out=tb0[:], in0=tb0[:], in1=tot_all[:, t0 + 3, :], op=ALU.add))
                    dv(lambda e, t0=t0: e.tensor_tensor(out=tb1[:], in0=tot_all[:, t0 + 4, :], in1=tot_all[:, t0 + 5, :], op=ALU.add))
                    dv(lambda e, t0=t0: e.tensor_tensor(out=tb1[:], in0=tb1[:], in1=tot_all[:, t0 + 6, :], op=ALU.add))
                    dv(lambda e, t0=t0: e.tensor_tensor(out=tb1[:], in0=tb1[:], in1=tot_all[:, t0 + 7, :], op=ALU.add))
                    for blk in range(2):
                        if blk == 0:
                            dv(lambda e: e.scalar_tensor_tensor(out=rb[:], in0=tb1[:], scalar=pm[:, 0:1], in1=carry[:],
                                                                op0=ALU.mult, op1=ALU.add))
                        else:
                            dv(lambda e: e.scalar_tensor_tensor(out=rb[:], in0=tb0[:], scalar=pm[:, 2:3], in1=carry[:],
                                                                op0=ALU.mult, op1=ALU.add))
                        for s4 in range(4):
                            tl = t0 + blk * 4 + s4
                            if blk == 0 and s4 == 2:
                                dv(lambda e, i=i: e.tensor_copy(out=cref[:, i, :], in_=rb[:]))
                            dv(lambda e, tl=tl: e.scalar_tensor_tensor(out=negc[:, tl, :], in0=cum_all[:, tl, :], scalar=-1.0,
                                                                       in1=rb[:], op0=ALU.mult, op1=ALU.subtract))
                            if s4 < 3:
                                dv(lambda e, tl=tl: e.tensor_tensor(out=rb[:], in0=rb[:], in1=tot_all[:, tl, :], op=ALU.add))
                    dv(lambda e: e.tensor_tensor(out=carry[:], in0=carry[:], in1=tb0[:], op=ALU.add))
                    dv(lambda e: e.tensor_tensor(out=carry[:], in0=carry[:], in1=tb1[:], op=ALU.add))

                KDBG = int(os.environ.get("KDBG", "99"))
                if KDBG <= 1:
                    fw.barrier()
                    return
                def f(e):
                    ins = None
                    for b in range(4):
                        pb = 64 * (b % 2)
                        nt_ = NKT + b // 2
                        for kt in range(PKT):
                            col = (b * PKT + kt) * 8
                            ins = e.matmul(banks[3][:, col:col + 8], lmat[:], clf_t[:, b, kt, :], start=True, stop=False)
                            for k2 in range(kt + 1, PKT):
                                ins = e.matmul(banks[3][:, col:col + 8], ones_f[:], clf_t[:, b, k2, :], start=False, stop=False)
                            hb = (b % 2) * 192
                            ins = e.matmul(banks[3][:, col:col + 8], selm[:, hb:hb + 128], lf_all[:, nt_, :],
                                           start=False, stop=True)
                    return ins
                fw.op("pe", f, reads=[Blf, BC, Bcs], writes=[PB[3]])

                def f(e):
                    ins = None
                    for b in range(4):
                        pb = 64 * (b % 2)
                        nt_ = NKT + b // 2
                        hb = (b % 2) * 192
                        ins = e.matmul(banks[4][0:64, b * 8:(b + 1) * 8], selm[:, hb + 128:hb + 192],
                                       lf_all[:, nt_, :], start=True, stop=True)
                    return ins
                fw.op("pe", f, reads=[Blf, BC], writes=[PB[4]])
                fw.op("dve", lambda e: e.tensor_copy(out=sbias_c[:], in_=banks[3][:, 0:4 * PKT * 8].rearrange("p (t h) -> p t h", h=8)),
                      reads=[PB[3]], writes=[Bcs])
                fw.op("dve", lambda e: e.tensor_copy(out=sbias_n[0:64, :, :], in_=banks[4][0:64, 0:32].rearrange("p (t h) -> p t h", h=8)),
                      reads=[PB[4]], writes=[Bcs])

                if KDBG <= 2:
                    fw.barrier()
                    return
                KTm = sb("KTm", [128, 2, SEQ], BF16); BKTm = Buf(); SKTm = fw.dsem("SKTm")
                Vm = sb("Vm", [128, NKT, 256], BF16); BVm = Buf(); SVm = fw.dsem("SVm")
                KTx = [sb(f"KTx{i}", [128, 2, 384], BF16) for i in range(2)]
                Vn = [sb(f"Vn{i}", [64, 4, 256], BF16) for i in range(2)]
                Vmeta = [sb(f"Vmeta{i}", [16, 256], BF16) for i in range(2)]
                Bmx = [Buf(), Buf()]; Smx = [fw.dsem(f"Smx{i}") for i in range(2)]
                Qt = [sb(f"Qt{i}", [128, 2, 512], BF16) for i in range(2)]; BQt = [Buf(), Buf()]
                SQt = [fw.dsem(f"SQt{i}") for i in range(2)]
                Pt = [sb(f"Pt{i}", [128, 512], BF16) for i in range(6)]; BPt = [Buf() for _ in range(6)]
                biasb = [sb(f"biasb{i}", [128, 2, NKT + 1], F32) for i in range(2)]; Bbias = [Buf(), Buf()]
                osb = sb("osb", [128, 2, 4, 256], F32); Bosb = Buf()
                lsb = sb("lsb", [128, 8], F32); rl = sb("rl", [128, 8], F32); nl4 = sb("nl4", [128, 4], F32); Bl = Buf()
                o1 = [sb(f"o1_{i}", [128, 256], F32) for i in range(2)]; Bo1 = [Buf(), Buf()]
                onb = [[sb(f"on{g}_{q}", [128, 256], BF16) for q in range(4)] for g in range(4)]
                Bonb = [[Buf() for q in range(4)] for g in range(4)]
                pending_tails = []

                def step_tick():
                    for t in pending_tails:
                        t[0] += 1
                    while pending_tails and pending_tails[0][0] >= 40:
                        pending_tails.pop(0)[1]()
                ss1 = [sb(f"ss1_{i}", [128, 2], F32) for i in range(2)]; Bss1 = [Buf(), Buf()]
                sqf = sb("sqf", [128, 256], F32); Bsqf = Buf()
                oTst = [sb(f"oTst{i}", [128, 2, 512], BF16) for i in range(4)]; BoTst = [Buf() for _ in range(4)]
                SoT = [fw.dsem(f"SoT{i}") for i in range(4)]
                ckb = [sb(f"ckb{i}", [128, PKT, 256], BF16) for i in range(2)]; Bckb = [Buf(), Buf()]
                cvb = [sb(f"cvb{i}", [128, PKT, 256], BF16) for i in range(2)]; Bcvb = [Buf(), Buf()]
                Sck = [fw.dsem(f"Sck{i}") for i in range(2)]; Scv = [fw.dsem(f"Scv{i}") for i in range(2)]
                cKT = [sb(f"cKT{i}", [128, 2, PAST], BF16) for i in range(2)]; BcKT = [Buf(), Buf()]
                gcnt = [0]
                tcnt = [0]
                scnt = [0]
                ecnt = [0]
                mA = lambda j: masks_b[:, j * 512:(j + 1) * 512]
                mB = lambda j: masks_b[:, 2048 + j * 512: 2048 + (j + 1) * 512]
                m_meta = masks_b[0:16, 4096:4112]
                m_new = masks_b[0:64, 4112:4176]

                def attend(u, nq, qcol0, tiles, extra_reads):
                    diff = u < 4
                    vw = 256 if diff else 128
                    gpar = gcnt[0] % 2
                    gcnt[0] += 1
                    fw.dma("sp", SQt[gpar], Qt[gpar][:, :, 0:nq],
                           qT_s[2 * u:2 * u + 2, :, qcol0:qcol0 + nq].rearrange("m d n -> d m n"), writes=[BQt[gpar]])
                    nqt = (nq + 127) // 128
                    steps = [(ti, m) for ti in range(len(tiles)) for m in range(2)]

                    def acc(m, qs, nqs):
                        return banks[m * 2 + qs // 2][0:nqs, (qs % 2) * 256:(qs % 2) * 256 + vw]

                    def qk(n):
                        ti, m = steps[n]
                        kT_fn, V_fn, nk, mask_fn, bias_fn = tiles[ti]
                        bk = 5 + scnt[0] % 3
                        mk = mask_fn(m)

                        def f(e):
                            ins = e.matmul(banks[bk][0:nk, 0:nq], kT_fn(m), Qt[gpar][:, m, 0:nq], start=True, stop=(mk is None))
                            if mk is not None:
                                ins = e.matmul(banks[bk][0:nk, 0:nq], ident_b[0:nk, 0:nk], mk, start=False, stop=True)
                            return ins
                        fw.op("pe", f, reads=[BQt[gpar], BC] + extra_reads, writes=[PB[bk]])
                        r = (bk, scnt[0] % 6)
                        scnt[0] += 1
                        return r

                    def ex(n, bk, ps):
                        ti, m = steps[n]
                        kT_fn, V_fn, nk, mask_fn, bias_fn = tiles[ti]
                        fw.op("act", lambda e: e.activation(out=Pt[ps][0:nk, 0:nq], in_=banks[bk][0:nk, 0:nq], func=AF.Exp,
                                                            bias=bias_fn(m), scale=SCALE),
                              reads=[PB[bk], Bbias[0], Bbias[1], Bcs, BC], writes=[BPt[ps]])

                    def pv(n, ps):
                        ti, m = steps[n]
                        kT_fn, V_fn, nk, mask_fn, bias_fn = tiles[ti]
                        first = (ti == 0)
                        last = (ti == len(tiles) - 1)

                        def f(e):
                            ins = None
                            for qs in range(nqt):
                                nqs = min(128, nq - qs * 128)
                                lhs = Pt[ps][0:nk, qs * 128:qs * 128 + nqs]
                                e.matmul(acc(m, qs, nqs), lhs, V_fn(m), start=False, stop=False)
                                ins = e.matmul(banks[4][0:nqs, m * 4 + qs:m * 4 + qs + 1], lhs, ones_b[0:nk, 0:1],
                                               start=False, stop=False)
                            return ins
                        fw.op("pe", f, reads=[BPt[ps], BC] + extra_reads, writes=[PB[2 * m], PB[2 * m + 1], PB[4]])

                    for t_ in pending_tails:
                        t_[0] += 1
                    while pending_tails and pending_tails[0][0] >= 2:
                        pending_tails.pop(0)[1]()
                    def fz(e):
                        ins = None
                        for bkz in ([0, 1, 2, 3, 4] if nqt > 2 else [0, 2, 4]):
                            ins = e.matmul(banks[bkz][:, :], zb[0:1, 0:128], zb[0:1, 0:512], start=True, stop=False)
                        return ins
                    fw.op("pe", fz, reads=[BC], writes=[PB[0], PB[1], PB[2], PB[3], PB[4]])

                    def fclose(e):
                        ins = None
                        for bkz in ([0, 1, 2, 3, 4] if nqt > 2 else [0, 2, 4]):
                            ins = e.matmul(banks[bkz][:, :], zb[0:1, 0:128], zb[0:1, 0:512], start=False, stop=True)
                        return ins
                    LA = 2
                    pend = [qk(n) for n in range(min(LA, len(steps)))]
                    for n in range(len(steps)):
                        cur = pend.pop(0)
                        if n + LA < len(steps):
                            pend.append(qk(n + LA))
                        ex(n, cur[0], cur[1])
                        pv(n, cur[1])
                    fw.op("pe", fclose, reads=[BC], writes=[PB[0], PB[1], PB[2], PB[3], PB[4]])
                    while len(pending_tails) >= 3:
                        pending_tails.pop(0)[1]()
                    R = min(128, nq)
                    for m in range(2):
                        for h in range((nqt + 1) // 2):
                            ncol = 512 if nqt >= 2 else 256
                            eng = "act" if (m + h) % 2 == 0 else "dve"
                            src = banks[m * 2 + h][0:R, 0:ncol]
                            dst = osb[0:R, m, 2 * h:2 * h + ncol // 256, :].rearrange("p a b -> p (a b)") if ncol == 512 else osb[0:R, m, 2 * h, :]
                            if eng == "act":
                                fw.op("act", lambda e, src=src, dst=dst: e.activation(out=dst, in_=src, func=AF.Copy),
                                      reads=[PB[m * 2 + h]], writes=[Bosb])
                            else:
                                fw.op("dve", lambda e, src=src, dst=dst: e.tensor_copy(out=dst, in_=src),
                                      reads=[PB[m * 2 + h]], writes=[Bosb])
                    fw.op("dve", lambda e: e.tensor_scalar(out=lsb[0:R, :], in0=banks[4][0:R, 0:8], scalar1=1e-37, scalar2=None, op0=ALU.max),
                          reads=[PB[4]], writes=[Bl])
                    fw.op("dve", lambda e: e.reciprocal(out=rl[0:R, :], in_=lsb[0:R, :]), reads=[Bl], writes=[Bl])
                    if diff:
                        fw.op("dve", lambda e: e.tensor_scalar(out=nl4[0:R, :], in0=rl[0:R, 4:8], scalar1=lamt[0:R, 1:2], scalar2=None,
                                                               op0=ALU.mult), reads=[Bl, BC], writes=[Bl])
                    oset = tcnt[0] % 4
                    tcnt[0] += 1
                    opar = oset
                    for qs in range(nqt):
                        on_q = onb[oset][qs]; Bon_q = Bonb[oset][qs]
                        nqs = min(128, nq - qs * 128)
                        ei = ecnt[0] % 2
                        ecnt[0] += 1
                        if diff:
                            fw.op("pool", lambda e, qs=qs, ei=ei, nqs=nqs: e.tensor_scalar(
                                out=o1[ei][0:nqs, :], in0=osb[0:nqs, 0, qs, :], scalar1=rl[0:nqs, qs:qs + 1], scalar2=None,
                                op0=ALU.mult), reads=[Bosb, Bl], writes=[Bo1[ei]])
                            fw.op("dve", lambda e, qs=qs, ei=ei, nqs=nqs: e.scalar_tensor_tensor(
                                out=o1[ei][0:nqs, :], in0=osb[0:nqs, 1, qs, :], scalar=nl4[0:nqs, qs:qs + 1], in1=o1[ei][0:nqs, :],
                                op0=ALU.mult, op1=ALU.add), reads=[Bosb, Bl, Bo1[ei]], writes=[Bo1[ei]])
                            fw.op("pool", lambda e, ei=ei, nqs=nqs: e.tensor_tensor(
                                out=sqf[0:nqs, :], in0=o1[ei][0:nqs, :], in1=o1[ei][0:nqs, :], op=ALU.mult),
                                reads=[Bo1[ei]], writes=[Bsqf])
                            fw.op("dve", lambda e, ei=ei, nqs=nqs: e.tensor_reduce(
                                out=ss1[ei][0:nqs, 0:1], in_=sqf[0:nqs, :], axis=AX.X, op=ALU.add),
                                reads=[Bsqf], writes=[Bss1[ei]])
                            fw.op("act", lambda e, ei=ei, nqs=nqs: e.activation(
                                out=ss1[ei][0:nqs, 0:1], in_=ss1[ei][0:nqs, 0:1], func=AF.Sqrt, bias=epsD[0:nqs, 2:3], scale=1.0),
                                reads=[Bss1[ei], BC], writes=[Bss1[ei]])
                            fw.op("dve", lambda e, ei=ei, nqs=nqs: e.reciprocal(out=ss1[ei][0:nqs, 0:1], in_=ss1[ei][0:nqs, 0:1]),
                                  reads=[Bss1[ei]], writes=[Bss1[ei]])
                            fw.op("dve", lambda e, ei=ei, nqs=nqs, on_q=on_q: e.scalar_tensor_tensor(
                                out=on_q[0:nqs, :], in0=o1[ei][0:nqs, :], scalar=ss1[ei][0:nqs, 0:1],
                                in1=hv[0:nqs, o["goa"]:o["goa"] + 256], op0=ALU.mult, op1=ALU.mult),
                                reads=[Bo1[ei], Bss1[ei], BC], writes=[Bon_q])
                        else:
                            for hd in range(2):
                                fw.op("pool", lambda e, qs=qs, ei=ei, nqs=nqs, hd=hd: e.tensor_scalar(
                                    out=o1[ei][0:nqs, hd * 128:(hd + 1) * 128], in0=osb[0:nqs, hd, qs, 0:128],
                                    scalar1=rl[0:nqs, hd * 4 + qs:hd * 4 + qs + 1], scalar2=None, op0=ALU.mult),
                                    reads=[Bosb, Bl], writes=[Bo1[ei]])
                            fw.op("pool", lambda e, ei=ei, nqs=nqs: e.tensor_tensor(
                                out=sqf[0:nqs, :], in0=o1[ei][0:nqs, :], in1=o1[ei][0:nqs, :], op=ALU.mult),
                                reads=[Bo1[ei]], writes=[Bsqf])
                            fw.op("dve", lambda e, ei=ei, nqs=nqs: e.tensor_reduce(
                                out=ss1[ei][0:nqs, 0:2], in_=sqf[0:nqs, :].rearrange("p (h d) -> p h d", d=128), axis=AX.X, op=ALU.add),
                                reads=[Bsqf], writes=[Bss1[ei]])
                            fw.op("act", lambda e, ei=ei, nqs=nqs: e.activation(
                                out=ss1[ei][0:nqs, 0:2], in_=ss1[ei][0:nqs, 0:2], func=AF.Sqrt, bias=epsD[0:nqs, 1:2], scale=1.0),
                                reads=[Bss1[ei], BC], writes=[Bss1[ei]])
                            fw.op("dve", lambda e, ei=ei, nqs=nqs: e.reciprocal(out=ss1[ei][0:nqs, 0:2], in_=ss1[ei][0:nqs, 0:2]),
                                  reads=[Bss1[ei]], writes=[Bss1[ei]])
                            for hd in range(2):
                                fw.op("dve", lambda e, ei=ei, nqs=nqs, hd=hd, on_q=on_q: e.scalar_tensor_tensor(
                                    out=on_q[0:nqs, hd * 128:(hd + 1) * 128], in0=o1[ei][0:nqs, hd * 128:(hd + 1) * 128],
                                    scalar=ss1[ei][0:nqs, hd:hd + 1], in1=hv[0:nqs, o["gob"]:o["gob"] + 128],
                                    op0=ALU.mult, op1=ALU.mult),
                                    reads=[Bo1[ei], Bss1[ei], BC], writes=[Bon_q])
                    def tail(u=u, nq=nq, nqt=nqt, oset=oset, opar=opar, qcol0=qcol0):
                        b7 = banks[7][:].bitcast(BF16)
                        for qs in range(nqt):
                            nqs = min(128, nq - qs * 128)
                            on_q = onb[oset][qs]; Bon_q = Bonb[oset][qs]

                            def f(e, nqs=nqs, on_q=on_q):
                                ins = None
                                for ch in range(2):
                                    ins = e.transpose(b7[:, ch * 128:ch * 128 + nqs], on_q[0:nqs, ch * 128:(ch + 1) * 128],
                                                      ident_b[0:nqs, 0:nqs])
                                return ins
                            fw.op("pe", f, reads=[Bon_q, BC], writes=[PB[7]])
                            fw.op("dve", lambda e, qs=qs, nqs=nqs: e.tensor_copy(
                                out=oTst[opar][:, :, qs * 128:qs * 128 + nqs],
                                in_=b7[:, 0:256].rearrange("p (c n) -> p c n", n=128)[:, :, 0:nqs]),
                                reads=[PB[7]], writes=[BoTst[opar]])
                        fw.dma("act", SoT[opar], oT_s[2 * u:2 * u + 2, :, qcol0:qcol0 + nq].rearrange("c d n -> d c n"),
                               oTst[opar][:, :, 0:nq], reads=[BoTst[opar]])
                    pending_tails.append([0, tail])

                def cache_load(u_, b_):
                    cb_ = b_ % 2
                    fw.dma("pool", Sck[cb_], ckb[cb_][:], ck_in[b_][:, 256 * u_:256 * u_ + 256].rearrange("(t p) e -> p t e", p=128),
                           writes=[Bckb[cb_]])
                    fw.dma("pool", Scv[cb_], cvb[cb_][:], cv_in[b_][:, 256 * u_:256 * u_ + 256].rearrange("(t p) e -> p t e", p=128),
                           writes=[Bcvb[cb_]])
                cache_load(0, 0)
                cache_load(0, 1)
                for u in range(8):
                    diff = u < 4
                    up = u % 2
                    vw = 256 if diff else 128
                    fw.dma("sp", SKTm, KTm[:], kT_s[2 * u:2 * u + 2, :, 0:SEQ].rearrange("m d n -> d m n"), writes=[BKTm])
                    for q4 in range(0, NKT, 16):
                        n4 = min(16, NKT - q4)
                        fw.dma("sp", SVm, Vm[:, q4:q4 + n4, :],
                               v_s[q4 * 128:(q4 + n4) * 128, 256 * u:256 * u + 256].rearrange("(t p) e -> p t e", p=128), writes=[BVm])
                    fw.dma("sp", Smx[up], KTx[up][:], kT_s[2 * u:2 * u + 2, :, SEQ:SEQ + 384].rearrange("m d n -> d m n"), writes=[Bmx[up]])
                    fw.dma("sp", Smx[up], Vn[up][:], v_s[SEQ:SEQ + 256, 256 * u:256 * u + 256].rearrange("(b k) e -> k b e", k=64),
                           writes=[Bmx[up]])
                    fw.dma("sp", Smx[up], Vmeta[up][:], v_s[SEQ + 256:SEQ + 272, 256 * u:256 * u + 256], writes=[Bmx[up]])
                    vsl = (lambda m: slice(0, 256)) if diff else (lambda m: slice(m * 128, (m + 1) * 128))
                    meta_tile_d = (lambda m, up=up: KTx[up][:, m, 256:272],
                                   lambda m, up=up, vsl=vsl: Vmeta[up][0:16, vsl(m)], 16)
                    for i in range(NPAIR):
                        nti = 8 * i + 8
                        bp = i % 2
                        if not diff:
                            for hd in range(2):
                                head = 2 * (u - 4) + hd
                                fw.op("dve", lambda e, hd=hd, head=head, bp=bp, nti=nti, i=i: e.tensor_scalar(
                                    out=biasb[bp][:, hd, 0:nti], in0=negc[:, 0:nti, head], scalar1=cref[:, i, head:head + 1],
                                    scalar2=None, op0=ALU.add), reads=[Bcs], writes=[Bbias[bp]])
                                fw.op("dve", lambda e, hd=hd, bp=bp, nti=nti: e.tensor_scalar(
                                    out=biasb[bp][:, hd, nti - 4:nti], in0=biasb[bp][:, hd, nti - 4:nti], scalar1=pm[:, 1:2],
                                    scalar2=None, op0=ALU.add), reads=[Bbias[bp], BC], writes=[Bbias[bp]])
                                fw.op("dve", lambda e, hd=hd, head=head, bp=bp, i=i: e.tensor_scalar(
                                    out=biasb[bp][0:16, hd, NKT:NKT + 1], in0=negc[0:16, NKT, head:head + 1],
                                    scalar1=cref[0:16, i, head:head + 1], scalar2=None, op0=ALU.add),
                                    reads=[Bcs], writes=[Bbias[bp]])
                        tiles = []
                        if diff:
                            tiles.append(meta_tile_d + (lambda m: None, lambda m: zcol[0:16, 0:1]))
                        else:
                            tiles.append(meta_tile_d + (lambda m: None, lambda m, bp=bp: biasb[bp][0:16, m, NKT:NKT + 1]))
                        for j in range(nti):
                            kf = lambda m, j=j: KTm[:, m, j * 128:(j + 1) * 128]
                            vf = lambda m, j=j, vsl=vsl: Vm[:, j, vsl(m)]
                            dj = j - 8 * i
                            if diff:
                                mf = (lambda m, dj=dj: mA(dj)) if 0 <= dj < 4 else (lambda m: None)
                                bf_ = (lambda m: pm[:, 1:2]) if dj >= 4 else (lambda m: zcol[:, 0:1])
                            else:
                                mf = (lambda m, dj=dj: mB(dj)) if 0 <= dj < 4 else (lambda m: None)
                                bf_ = lambda m, bp=bp, j=j: biasb[bp][:, m, j:j + 1]
                            tiles.append((kf, vf, 128, mf, bf_))
                        attend(u, 512, i * 512, tiles, [BKTm, BVm, Bmx[up]])
                    if KDBG <= 3:
                        break
                    if diff:
                        tiles = [meta_tile_d + (lambda m: None, lambda m: zcol[0:16, 0:1])]
                    else:
                        tiles = [meta_tile_d + (lambda m: m_meta, lambda m, u=u: metab[0:16, 2 * (u - 4) + m:2 * (u - 4) + m + 1])]
                    attend(u, 16, NPAIR * 512 + 256, tiles, [Bmx[up]])
                    if KDBG <= 4:
                        break
                    def cache_T(b, u=u):
                        cb = b % 2
                        b7 = banks[7][:].bitcast(BF16)
                        for m in range(2):
                            for k0 in range(0, PKT, 8):
                                n8 = min(8, PKT - k0)

                                def f(e, m=m, k0=k0, n8=n8, cb=cb):
                                    ins = None
                                    for k in range(n8):
                                        ins = e.transpose(b7[:, k * 128:(k + 1) * 128], ckb[cb][:, k0 + k, m * 128:(m + 1) * 128], ident_b[:])
                                    return ins
                                fw.op("pe", f, reads=[Bckb[cb], BC], writes=[PB[7]])
                                fw.op("dve", lambda e, m=m, k0=k0, n8=n8, cb=cb: e.tensor_copy(
                                    out=cKT[cb][:, m, k0 * 128:(k0 + n8) * 128], in_=b7[:, 0:n8 * 128]),
                                    reads=[PB[7]], writes=[BcKT[cb]])
                    cache_T(0)
                    for b in range(4):
                        cb = b % 2
                        if b + 1 < 4:
                            cache_T(b + 1)
                        tiles = []
                        for kt in range(PKT):
                            kf = lambda m, kt=kt, cb=cb: cKT[cb][:, m, kt * 128:(kt + 1) * 128]
                            vf = lambda m, kt=kt, cb=cb, vsl=vsl: cvb[cb][:, kt, vsl(m)]
                            if diff:
                                bf_ = lambda m: zcol[:, 0:1]
                            else:
                                bf_ = lambda m, b=b, kt=kt, u=u: sbias_c[:, b * PKT + kt, 2 * (u - 4) + m:2 * (u - 4) + m + 1]
                            tiles.append((kf, vf, 128, lambda m: None, bf_))
                        kf = lambda m, b=b, up=up: KTx[up][:, m, 64 * b:64 * b + 64]
                        vf = lambda m, b=b, up=up, vsl=vsl: Vn[up][0:64, b, vsl(m)]
                        if diff:
                            tiles.append((kf, vf, 64, lambda m: None, lambda m: zcol[0:64, 0:1]))
                        else:
                            tiles.append((kf, vf, 64, lambda m: m_new,
                                          lambda m, b=b, u=u: sbias_n[0:64, b, 2 * (u - 4) + m:2 * (u - 4) + m + 1]))
                        attend(u, 64, NPAIR * 512 + 64 * b, tiles, [BcKT[cb], Bcvb[cb], Bmx[up]])
                        if b + 2 < 4:
                            cache_load(u, b + 2)
                        elif u + 1 < 8:
                            cache_load(u + 1, b - 2)
                while pending_tails:
                    pending_tails.pop(0)[1]()
                fw.barrier()

        def phaseC():
            with contextlib.ExitStack() as ph:
                c = Ctx(); c.tag = "C"
                alloc_common(ph, c)
                sb = c.sb
                oTt = c.big[:, 0:16 * 512].rearrange("p (f n) -> p f n", n=512); BoTt = Buf(); SoTt = fw.dsem("SoTt")
                ystage = sb("ystage", [128, D], F32); Bys = Buf(); Sys = fw.dsem("Sys")
                wosl = [sb(f"wosl{i}", [128, 16, 128], BF16) for i in range(2)]; Bwosl = [Buf(), Buf()]
                Swosl = [fw.dsem(f"Swosl{i}") for i in range(2)]
                Sx = fw.dsem("Sxc")
                for oi in range(NPAIR + 1):
                    misc = (oi == NPAIR)
                    nt = 3 if misc else 4
                    N = nt * 128
                    orow0 = oi * 512
                    fw.dma("sp", Sx, c.xT[:, :, 0:N], x1_s[oi][:, :, 0:N], writes=c.BxT)
                    fw.dma("sp", SoTt, oTt[:, :, 0:N], oT_s[:, :, orow0:orow0 + N].rearrange("c d n -> d c n"),
                           writes=[BoTt] + c.BgT)
                    for dc in range(DC):
                        sl = dc % 2
                        fw.dma("sp", Swosl[sl], wosl[sl][:], wout_s[dc], reads=list(Bwout[dc]) if isinstance(Bwout[dc], tuple) else [Bwout[dc]], writes=[Bwosl[sl]])

                        def f(e, sl=sl, dc=dc):
                            ins = None
                            for fc in range(16):
                                ins = e.matmul(banks[4 + dc % 2][:, 0:N], wosl[sl][:, fc, :], oTt[:, fc, 0:N],
                                               start=(fc == 0), stop=(fc == 15))
                            return ins
                        fw.op("pe", f, reads=[BoTt, Bwosl[sl]], writes=[PB[4 + dc % 2]])
                        fw.op("dve", lambda e, dc=dc: e.tensor_tensor(out=c.xT[:, dc, 0:N], in0=banks[4 + dc % 2][:, 0:N],
                                                                      in1=c.xT[:, dc, 0:N], op=ALU.add),
                              reads=[PB[4 + dc % 2], c.BxT[dc]], writes=[c.BxT[dc]])
                    for gb in c.BgT:
                        for k2, tk in BoTt.r.items():
                            gb.r[(k2, "oTt")] = tk
                    rmsnorm_fm(c, N, 2 * DC)
                    ffn(c, N, w1b_s, w3b_s, w2b_s, Bw1b, Bw3b, Bw2b)
                    rmsnorm_fm(c, N, 3 * DC, final=True)
                    for ts in range(nt):
                        for q4 in range((DC + 3) // 4):
                            nch = min(4, DC - q4 * 4)

                            def f(e, q4=q4, nch=nch, ts=ts):
                                ins = None
                                for k in range(nch):
                                    ch = q4 * 4 + k
                                    ins = e.transpose(banks[q4 % 4][:, k * 128:(k + 1) * 128],
                                                      c.xT[:, ch, ts * 128:(ts + 1) * 128], ident_f[:])
                                return ins
                            fw.op("pe", f, reads=c.BxT[q4 * 4:q4 * 4 + nch] + [BC], writes=[PB[q4 % 4]])
                            if q4 % 2 == 0:
                                fw.op("act", lambda e, q4=q4, nch=nch: e.activation(
                                    out=ystage[:, q4 * 512:q4 * 512 + nch * 128], in_=banks[q4 % 4][:, 0:nch * 128], func=AF.Copy),
                                    reads=[PB[q4 % 4]], writes=[Bys])
                            else:
                                fw.op("dve", lambda e, q4=q4, nch=nch: e.tensor_copy(
                                    out=ystage[:, q4 * 512:q4 * 512 + nch * 128], in_=banks[q4 % 4][:, 0:nch * 128]),
                                    reads=[PB[q4 % 4]], writes=[Bys])
                        fw.dma("act", Sys, y_out[orow0 + ts * 128: orow0 + (ts + 1) * 128, :], ystage[:], reads=[Bys])
                fw.barrier()

        phaseA()
        if phases >= 2:
            phaseB()
        if phases >= 3:
            phaseC()
        fw.barrier()
    return nc


def _consts():
    ident = np.eye(128, dtype=np.float32)
    kk = np.arange(128)
    umat = (kk[:, None] <= kk[None, :]).astype(np.float32)
    lmat = (kk[:, None] > kk[None, :]).astype(np.float32)
    masks = np.zeros((128, MASK_COLS), np.float32)
    q = np.arange(512)
    for j in range(4):
        kpos = j * 128 + kk
        mA = np.where((kpos[:, None] // 64) <= (q[None, :] // 64), 0.0, NEGM)
        mB = np.where(kpos[:, None] <= q[None, :], 0.0, NEGM)
        masks[:, j * 512:(j + 1) * 512] = mA
        masks[:, 2048 + j * 512: 2048 + (j + 1) * 512] = mB
    k16 = np.arange(16)
    masks[:16, 4096:4112] = np.where(k16[:, None] <= k16[None, :], 0.0, NEGM)
    k64 = np.arange(64)
    masks[:64, 4112:4176] = np.where(k64[:, None] <= k64[None, :], 0.0, NEGM)
    sel = np.zeros((128, 384), np.float32)
    for h in range(2):
        rows = np.arange(64 * h, 64 * h + 64)
        sel[rows, h * 192: h * 192 + 128] = 1.0
        sel[rows, h * 192 + 128: h * 192 + 192] = (np.arange(64)[:, None] > np.arange(64)[None, :]).astype(np.float32)
    return ident, umat, lmat, masks, sel


def _rope_table(pos):
    half = 16
    inv = np.power(np.float32(500000.0), -np.arange(half, dtype=np.float32) * np.float32(2.0) / np.float32(32)).astype(np.float32)
    ang = pos.astype(np.float32)[:, None] * inv[None, :]
    return np.concatenate([np.cos(ang), np.sin(ang)], axis=1).astype(np.float32)


def make_in_maps(cfg, inp):
    D, SEQ, PAST, DC = cfg.D, cfg.SEQ, cfg.PAST, cfg.DC
    ident, umat, lmat, masks, sel = _consts()
    f = lambda a: np.ascontiguousarray(np.asarray(a, dtype=np.float32))
    gains = np.concatenate([f(inp[k])[0].reshape(DC, 128).T for k in ("g_ffn1", "g_mix", "g_ffn2", "g_final")], axis=1)
    gains = np.ascontiguousarray(gains)
    hvec = np.concatenate([f(inp[k])[0] for k in ("g_qa", "g_ka", "g_qb", "g_kb", "g_oa", "g_ob", "b_f",
                                                   "lambda_q1", "lambda_k1", "lambda_q2", "lambda_k2")])[None, :]
    hvec = np.ascontiguousarray(hvec)
    assert hvec.shape[1] == HV_LEN
    common = dict(ident=ident, umat=umat, lmat=lmat, masks=masks, sel=sel, gains=gains, hvec=hvec,
                  w1a=f(inp["ffn1_w1"])[0], w3a=f(inp["ffn1_w3"])[0], w2a=f(inp["ffn1_w2"])[0],
                  w1b=f(inp["ffn2_w1"])[0], w3b=f(inp["ffn2_w3"])[0], w2b=f(inp["ffn2_w2"])[0],
                  win=f(inp["w_in"])[0], wout=f(inp["w_out"])[0])
    xp = f(inp["x_prompt"]); xsm = f(inp["x_sample"]); meta = f(inp["meta_tokens"])
    cak = f(inp["cache_a_k"])[0]; cav = f(inp["cache_a_v"])[0]
    cbk = f(inp["cache_b_k"])[0]; cbv = f(inp["cache_b_v"])[0]; clf = f(inp["cache_b_logf"])[0]
    maps = []
    for core in range(8):
        s, p = core // 2, core % 2
        order = []
        for i in range(cfg.NPAIR):
            order += [2 * i + p, 2 * i + 1 - p]
        xin = np.zeros((cfg.NROWS, D), np.float32)
        pos = np.zeros((cfg.NROWS,), np.float32)
        for j, tb in enumerate(order):
            xin[j * 512:(j + 1) * 512] = xp[s, tb * 512:(tb + 1) * 512]
            pos[j * 512:(j + 1) * 512] = 16 + tb * 512 + np.arange(512)
        xin[SEQ:SEQ + 256] = xsm[4 * core:4 * core + 4].reshape(256, D)
        pos[SEQ:SEQ + 256] = np.tile(PAST + np.arange(64), 4)
        xin[SEQ + 256:SEQ + 272] = meta
        pos[SEQ + 256:SEQ + 272] = np.arange(16)
        pmv = np.zeros((128, 4), np.float32)
        pmv[:, 0] = p
        pmv[:, 1] = NEGM if p == 0 else 0.0
        pmv[:, 2] = 1 - p
        ck = np.concatenate([cak[4 * core:4 * core + 4].reshape(4, PAST, 1024),
                             cbk[4 * core:4 * core + 4].reshape(4, PAST, 1024)], axis=2)
        cv = np.concatenate([cav[4 * core:4 * core + 4].reshape(4, PAST, 1024),
                             cbv[4 * core:4 * core + 4].reshape(4, PAST, 1024)], axis=2)
        m = dict(common)
        m.update(xin=xin, rope=_rope_table(pos), pm=pmv, ck=np.ascontiguousarray(ck), cv=np.ascontiguousarray(cv),
                 clf=np.ascontiguousarray(clf[4 * core:4 * core + 4]))
        maps.append(m)
    return maps


def assemble(cfg, results, batch=4, dec_batch=32):
    D, SEQ = cfg.D, cfg.SEQ
    L = SEQ + 16
    NP = cfg.NPAIR
    y_p = np.zeros((batch, SEQ, D), np.float32)
    y_s = np.zeros((dec_batch, 64, D), np.float32)
    pk = {k: np.zeros((1, batch, L, w), np.float32) for k, w in (("ak", 1024), ("av", 1024), ("bk", 1024), ("bv", 1024), ("lf", 8))}
    sk = {k: np.zeros((1, dec_batch, 64, w), np.float32) for k, w in (("ak", 1024), ("av", 1024), ("bk", 1024), ("bv", 1024), ("lf", 8))}
    for core in range(8):
        s, p = core // 2, core % 2
        r = results[core]
        for i in range(NP):
            tb = 2 * i + p
            y_p[s, tb * 512:(tb + 1) * 512] = r["y_out"][i * 512:(i + 1) * 512]
            for k in pk:
                pk[k][0, s, 16 + tb * 512:16 + (tb + 1) * 512] = r[k + "_out"][i * 512:(i + 1) * 512]
        m0 = NP * 512
        y_s[4 * core:4 * core + 4] = r["y_out"][m0:m0 + 256].reshape(4, 64, D)
        for k in sk:
            sk[k][0, 4 * core:4 * core + 4] = r[k + "_out"][m0:m0 + 256].reshape(4, 64, -1)
        if p == 0:
            for k in pk:
                pk[k][0, s, 0:16] = r[k + "_out"][m0 + 256:m0 + 272]
    return (y_p, y_s,
            pk["ak"].reshape(1, batch, L, 4, 2, 128), pk["av"].reshape(1, batch, L, 4, 256),
            pk["bk"].reshape(1, batch, L, 8, 128), pk["bv"].reshape(1, batch, L, 8, 128), pk["lf"],
            sk["ak"].reshape(1, dec_batch, 64, 4, 2, 128), sk["av"].reshape(1, dec_batch, 64, 4, 256),
            sk["bk"].reshape(1, dec_batch, 64, 8, 128), sk["bv"].reshape(1, dec_batch, 64, 8, 128), sk["lf"])


def run(cfg, inp, phases=3, trace=False):
    nc = build(cfg, phases)
    maps = make_in_maps(cfg, inp)
    res = run_bass_kernel_spmd(nc, maps, core_ids=list(range(8)), trace=trace)
    return assemble(cfg, res.results), res


def kernel(**inputs):
    cfg = Cfg()
    out, _ = run(cfg, inputs)
    return out
```

```python
import contextlib
import os
import numpy as np
import concourse.bass as bass
import concourse.mybir as mybir
from concourse.bass_utils import run_bass_kernel_spmd

F32 = mybir.dt.float32
BF16 = mybir.dt.bfloat16
AF = mybir.ActivationFunctionType
ALU = mybir.AluOpType
AX = mybir.AxisListType
EPS = 1e-6
NEGM = -30000.0
SCALE = 128 ** -0.5
N_IN = 6152
LAM_INIT = 0.2
HV_OFF = dict(gqa=0, gka=128, gqb=256, gkb=384, goa=512, gob=768, bf=896, lq1=904, lk1=1032, lq2=1160, lk2=1288)
HV_LEN = 1416
MASK_COLS = 4176


class Cfg:
    def __init__(s, D=2048, DFF=5632, SEQ=8192, PAST=2048):
        s.D, s.DFF, s.SEQ, s.PAST = D, DFF, SEQ, PAST
        s.DC = D // 128
        s.FC = DFF // 128
        s.NFG = DFF // 256
        s.NBLK = SEQ // 512
        s.NPAIR = s.NBLK // 2
        s.NROWS = SEQ + 384
        s.NQ = s.NPAIR * 512 + 384
        s.NKT = SEQ // 128
        s.PKT = PAST // 128


class Buf:
    __slots__ = ("w", "r")

    def __init__(s):
        s.w = None
        s.r = {}


class DSem:
    def __init__(s, sem):
        s.sem = sem
        s.count = 0


class FW:
    def __init__(s, nc, es):
        s.nc = nc
        s.es = es
        s.eng = {"pe": nc.tensor, "act": nc.scalar, "dve": nc.vector, "pool": nc.gpsimd, "sp": nc.sync}
        s.sem = {k: es.enter_context(nc.semaphore("sem_" + k)) for k in s.eng}
        s.cnt = {k: 0 for k in s.eng}
        s.waited = {k: {} for k in s.eng}
        s.dsems = []

    def dsem(s, name):
        d = DSem(s.es.enter_context(s.nc.semaphore(name)))
        s.dsems.append(d)
        return d

    def wait(s, e, tk):
        if tk is None:
            return
        sem, val = tk
        k = id(sem)
        if s.waited[e].get(k, 0) >= val:
            return
        s.eng[e].wait_ge(sem, val)
        s.waited[e][k] = val

    def deps(s, e, reads, writes):
        own = s.sem[e]
        for b in reads:
            s.wait(e, b.w)
        for b in writes:
            if b.w is not None and b.w[0] is not own:
                s.wait(e, b.w)
            for tk in b.r.values():
                if tk[0] is not own:
                    s.wait(e, tk)

    def done(s, tk, reads, writes):
        k = id(tk[0])
        for b in reads:
            o = b.r.get(k)
            if o is None or o[1] < tk[1]:
                b.r[k] = tk
        for b in writes:
            b.w = tk
            b.r = {}

    def op(s, e, fn, reads=(), writes=()):
        s.deps(e, reads, writes)
        ins = fn(s.eng[e])
        s.cnt[e] += 1
        ins.then_inc(s.sem[e], 1)
        tk = (s.sem[e], s.cnt[e])
        s.done(tk, reads, writes)
        return tk

    def dma(s, e, ds, out, in_, reads=(), writes=()):
        s.deps(e, reads, writes)
        s.eng[e].dma_start(out=out, in_=in_).then_inc(ds.sem, 16)
        ds.count += 16
        tk = (ds.sem, ds.count)
        s.done(tk, reads, writes)
        return tk

    def barrier(s):
        for e in s.eng:
            for e2 in s.eng:
                if s.cnt[e2] > 0:
                    s.wait(e, (s.sem[e2], s.cnt[e2]))
            for d in s.dsems:
                if d.count > 0:
                    s.wait(e, (d.sem, d.count))


def build(cfg, phases=3):
    D, DFF, SEQ, PAST = cfg.D, cfg.DFF, cfg.SEQ, cfg.PAST
    DC, FC, NFG, NBLK, NPAIR = cfg.DC, cfg.FC, cfg.NFG, cfg.NBLK, cfg.NPAIR
    NROWS, NQ, NKT, PKT = cfg.NROWS, cfg.NQ, cfg.NKT, cfg.PKT
    nc = bass.Bass("TRN2", target_bir_lowering=False)

    def din(name, shape):
        return nc.dram_tensor(name, list(shape), F32, kind="ExternalInput").ap()

    def dout(name, shape):
        return nc.dram_tensor(name, list(shape), F32, kind="ExternalOutput").ap()

    def dscr(name, shape, dt):
        return nc.dram_tensor(name, list(shape), dt).ap()

    xin = din("xin", [NROWS, D])
    rope_in = din("rope", [NROWS, 32])
    ident_in = din("ident", [128, 128])
    umat_in = din("umat", [128, 128])
    lmat_in = din("lmat", [128, 128])
    sel_in = din("sel", [128, 384])
    masks_in = din("masks", [128, MASK_COLS])
    pm_in = din("pm", [128, 4])
    gains_in = din("gains", [128, 4 * DC])
    hvec_in = din("hvec", [1, HV_LEN])
    ck_in = din("ck", [4, PAST, 2048])
    cv_in = din("cv", [4, PAST, 2048])
    clf_in = din("clf", [4, PAST, 8])
    w1a = din("w1a", [D, DFF]); w3a = din("w3a", [D, DFF]); w2a = din("w2a", [DFF, D])
    w1b = din("w1b", [D, DFF]); w3b = din("w3b", [D, DFF]); w2b = din("w2b", [DFF, D])
    win = din("win", [D, N_IN]); wout = din("wout", [2048, D])

    y_out = dout("y_out", [NQ, D])
    ak_out = dout("ak_out", [NQ, 1024]); av_out = dout("av_out", [NQ, 1024])
    bk_out = dout("bk_out", [NQ, 1024]); bv_out = dout("bv_out", [NQ, 1024])
    lf_out = dout("lf_out", [NQ, 8])

    w1a_s = dscr("w1a_s", [NFG, 128, DC, 256], BF16); w3a_s = dscr("w3a_s", [NFG, 128, DC, 256], BF16)
    w1b_s = dscr("w1b_s", [NFG, 128, DC, 256], BF16); w3b_s = dscr("w3b_s", [NFG, 128, DC, 256], BF16)
    w2a_s = dscr("w2a_s", [DC, 128, FC, 128], BF16); w2b_s = dscr("w2b_s", [DC, 128, FC, 128], BF16)
    win_s = dscr("win_s", [24, 128, DC, 256], BF16)
    wfb_s = dscr("wfb_s", [128, DC, 8], BF16)
    wout_s = dscr("wout_s", [DC, 128, 16, 128], BF16)
    kT_s = dscr("kT_s", [16, 128, NROWS], BF16)
    qT_s = dscr("qT_s", [16, 128, NQ], BF16)
    v_s = dscr("v_s", [NROWS, 2048], BF16)
    oT_s = dscr("oT_s", [16, 128, NQ], BF16)
    x1_s = dscr("x1_s", [NPAIR + 1, 128, DC, 512], F32)

    es = contextlib.ExitStack()
    with es:
        fw = FW(nc, es)

        def sbg(name, shape, dt):
            return es.enter_context(nc.sbuf_tensor("g_" + name, list(shape), dt))

        banks = [es.enter_context(nc.psum_tensor(f"bank{i}", [128, 512], F32)) for i in range(8)]
        PB = [Buf() for _ in range(8)]

        ident_f = sbg("ident_f", [128, 128], F32); ident_b = sbg("ident_b", [128, 128], BF16)
        ones_f = sbg("ones_f", [128, 128], F32); ones_b = sbg("ones_b", [128, 128], BF16)
        umat = sbg("umat", [128, 128], F32); lmat = sbg("lmat", [128, 128], F32)
        masks_b = sbg("masks_b", [128, MASK_COLS], BF16)
        pm = sbg("pm", [128, 4], F32)
        gains = sbg("gains", [128, 4 * DC], F32)
        hv = sbg("hv", [128, HV_LEN], F32)
        lamt = sbg("lamt", [128, 8], F32)
        lamtmp = sbg("lamtmp", [128, 128], F32)
        lf_all = sbg("lf_all", [128, NKT + 3, 8], F32)
        BC = Buf()
        Blf = Buf()
        cs = fw.dsem("cs")
        fw.dma("sp", cs, ident_f[:], ident_in, writes=[BC])
        fw.dma("sp", cs, umat[:], umat_in, writes=[BC])
        fw.dma("sp", cs, lmat[:], lmat_in, writes=[BC])
        selm = sbg("selm", [128, 384], F32)
        fw.dma("sp", cs, selm[:], sel_in, writes=[BC])
        fw.dma("sp", cs, pm[:], pm_in, writes=[BC])
        fw.dma("sp", cs, gains[:], gains_in, writes=[BC])
        fw.dma("sp", cs, hv[:], hvec_in.partition_broadcast(128), writes=[BC])
        mcs = fw.dsem("mcs")
        fw.dma("pool", mcs, masks_b[:], masks_in, writes=[BC])
        fw.op("dve", lambda e: e.tensor_copy(out=ident_b[:], in_=ident_f[:]), reads=[BC], writes=[BC])
        fw.op("dve", lambda e: e.memset(ones_f[:], 1.0), writes=[BC])
        fw.op("dve", lambda e: e.memset(ones_b[:], 1.0), writes=[BC])
        zb = sbg("zb", [1, 512], BF16)
        fw.op("dve", lambda e: e.memset(zb[:], 0.0), writes=[BC])
        fw.op("dve", lambda e: e.tensor_scalar(out=gains[:], in0=gains[:], scalar1=float(D) ** 0.5, scalar2=None,
                                               op0=ALU.mult), reads=[BC], writes=[BC])
        o = HV_OFF
        fw.op("dve", lambda e: e.tensor_scalar(out=hv[:, 0:512], in0=hv[:, 0:512], scalar1=128.0 ** 0.5, scalar2=None,
                                               op0=ALU.mult), reads=[BC], writes=[BC])
        fw.op("dve", lambda e: e.tensor_scalar(out=hv[:, 512:768], in0=hv[:, 512:768],
                                               scalar1=(256.0 ** 0.5) * (1.0 - LAM_INIT), scalar2=None,
                                               op0=ALU.mult), reads=[BC], writes=[BC])
        fw.op("dve", lambda e: e.tensor_scalar(out=hv[:, 768:896], in0=hv[:, 768:896], scalar1=128.0 ** 0.5,
                                               scalar2=None, op0=ALU.mult), reads=[BC], writes=[BC])
        fw.op("dve", lambda e: e.tensor_tensor(out=lamtmp[:], in0=hv[:, o["lq1"]:o["lq1"] + 128],
                                               in1=hv[:, o["lk1"]:o["lk1"] + 128], op=ALU.mult), reads=[BC], writes=[BC])
        fw.op("dve", lambda e: e.tensor_reduce(out=lamt[:, 2:3], in_=lamtmp[:], axis=AX.X, op=ALU.add), reads=[BC], writes=[BC])
        fw.op("dve", lambda e: e.tensor_tensor(out=lamtmp[:], in0=hv[:, o["lq2"]:o["lq2"] + 128],
                                               in1=hv[:, o["lk2"]:o["lk2"] + 128], op=ALU.mult), reads=[BC], writes=[BC])
        fw.op("dve", lambda e: e.tensor_reduce(out=lamt[:, 3:4], in_=lamtmp[:], axis=AX.X, op=ALU.add), reads=[BC], writes=[BC])
        fw.op("act", lambda e: e.activation(out=lamt[:, 2:4], in_=lamt[:, 2:4], func=AF.Exp), reads=[BC], writes=[BC])
        fw.op("dve", lambda e: e.tensor_tensor(out=lamt[:, 0:1], in0=lamt[:, 2:3], in1=lamt[:, 3:4], op=ALU.subtract),
              reads=[BC], writes=[BC])
        fw.op("dve", lambda e: e.tensor_scalar(out=lamt[:, 0:1], in0=lamt[:, 0:1], scalar1=LAM_INIT, scalar2=None,
                                               op0=ALU.add), reads=[BC], writes=[BC])
        fw.op("dve", lambda e: e.tensor_scalar(out=lamt[:, 1:2], in0=lamt[:, 0:1], scalar1=-1.0, scalar2=None,
                                               op0=ALU.mult), reads=[BC], writes=[BC])
        fw.op("dve", lambda e: e.memset(lf_all[:], 0.0), writes=[Blf])
        epsD = sbg("epsD", [128, 4], F32)
        fw.op("dve", lambda e: e.memset(epsD[:, 0:1], float(D) * EPS), writes=[BC])
        fw.op("dve", lambda e: e.memset(epsD[:, 1:2], 128.0 * EPS), writes=[BC])
        fw.op("dve", lambda e: e.memset(epsD[:, 2:3], 256.0 * EPS), writes=[BC])
        fw.op("dve", lambda e: e.memset(epsD[:, 3:4], 1.0), writes=[BC])

        cast_sems = [fw.dsem(f"cast{i}") for i in range(6)]
        cast_n = [0]

        def cast(out_ap, in_ap, buf):
            ds = cast_sems[cast_n[0] % 6]
            if ds.count > 0:
                fw.wait("pool", (ds.sem, ds.count))
            fw.dma("pool", ds, out_ap, in_ap, writes=[buf])
            cast_n[0] += 1

        def cast_up(ws, w):
            bufs = []
            wv = w.rearrange("(c p) (g j) -> g p c j", p=128, j=256)
            for g in range(NFG):
                b = Buf(); bufs.append(b)
                cast(ws[g], wv[g], b)
            return bufs

        def cast_down(ws, w, nfc):
            bufs = []
            wv = w.rearrange("(f p) (c j) -> c p f j", p=128, j=128)
            for c in range(DC):
                b = Buf(); bufs.append(b)
                h = (nfc + 1) // 2
                cast(ws[c][:, 0:h, :], wv[c][:, 0:h, :], b)
                if nfc > h:
                    b2 = Buf()
                    cast(ws[c][:, h:nfc, :], wv[c][:, h:nfc, :], b2)
                    bufs[-1] = (b, b2)
            return bufs

        Bw1a = []; Bw3a = []
        w1v = w1a.rearrange("(c p) (g j) -> g p c j", p=128, j=256)
        w3v = w3a.rearrange("(c p) (g j) -> g p c j", p=128, j=256)
        for g in range(NFG):
            b1 = Buf(); b3 = Buf(); Bw1a.append(b1); Bw3a.append(b3)
            cast(w1a_s[g], w1v[g], b1)
            cast(w3a_s[g], w3v[g], b3)
        Bw2a = cast_down(w2a_s, w2a, FC)
        Bwin = []
        for g in range(24):
            b = Buf(); Bwin.append(b)
            cast(win_s[g], win[:, g * 256:(g + 1) * 256].rearrange("(c p) j -> p c j", p=128), b)
        b = Buf(); Bwin.append(b)
        cast(wfb_s, win[:, 6144:6152].rearrange("(c p) j -> p c j", p=128), b)
        wfb = sbg("wfb", [128, DC, 8], BF16)
        Swfb = fw.dsem("Swfb")
        Bwfb = Buf()
        Bwfb_cast = b
        Bw1b = cast_up(w1b_s, w1b)
        Bw3b = cast_up(w3b_s, w3b)
        Bw2b = cast_down(w2b_s, w2b, FC)
        Bwout = cast_down(wout_s, wout, 16)

        class Ctx:
            pass

        def rmsnorm_fm(c, N, gcol0, final=False):
            for ch in range(DC):
                if ch % 2 == 0:
                    fw.op("act", lambda e, ch=ch: e.activation(out=c.hT[:, ch, 0:N], in_=c.xT[:, ch, 0:N], func=AF.Square),
                          reads=[c.BxT[ch]], writes=[c.BhT[ch]])
                else:
                    fw.op("dve", lambda e, ch=ch: e.tensor_tensor(out=c.hT[:, ch, 0:N], in0=c.xT[:, ch, 0:N],
                                                                  in1=c.xT[:, ch, 0:N], op=ALU.mult),
                          reads=[c.BxT[ch]], writes=[c.BhT[ch]])

            def f(e):
                ins = None
                for ch in range(DC):
                    ins = e.matmul(banks[6][:, 0:N], ones_b[:], c.hT[:, ch, 0:N], start=(ch == 0), stop=(ch == DC - 1))
                return ins
            fw.op("pe", f, reads=c.BhT + [BC], writes=[PB[6]])
            fw.op("act", lambda e: e.activation(out=c.rstd[:, 0:N], in_=banks[6][:, 0:N], func=AF.Sqrt,
                                                bias=epsD[:, 0:1], scale=1.0),
                  reads=[PB[6], BC], writes=[c.Brstd])
            fw.op("dve", lambda e: e.reciprocal(out=c.rstd[:, 0:N], in_=c.rstd[:, 0:N]),
                  reads=[c.Brstd], writes=[c.Brstd])
            for ch in range(DC):
                eng = "dve"
                if final:
                    fw.op(eng, lambda e, ch=ch: e.scalar_tensor_tensor(
                        out=c.xT[:, ch, 0:N], in0=c.xT[:, ch, 0:N], scalar=gains[:, gcol0 + ch:gcol0 + ch + 1],
                        in1=c.rstd[:, 0:N], op0=ALU.mult, op1=ALU.mult),
                        reads=[c.BxT[ch], c.Brstd, BC, c.BhT[ch]], writes=[c.BxT[ch]])
                else:
                    fw.op(eng, lambda e, ch=ch: e.scalar_tensor_tensor(
                        out=c.hT[:, ch, 0:N], in0=c.xT[:, ch, 0:N], scalar=gains[:, gcol0 + ch:gcol0 + ch + 1],
                        in1=c.rstd[:, 0:N], op0=ALU.mult, op1=ALU.mult),
                        reads=[c.BxT[ch], c.Brstd, BC], writes=[c.BhT[ch]])

        def ffn(c, N, w1s, w3s, w2s, Bw1, Bw3, Bw2):
            for g in range(NFG):
                sa = (g % 2) * 2
                fw.dma("sp", c.Swsl[sa], c.wsl[sa][:], w1s[g], reads=[Bw1[g]], writes=[c.Bwsl[sa]])
                fw.dma("sp", c.Swsl[sa + 1], c.wsl[sa + 1][:], w3s[g], reads=[Bw3[g]], writes=[c.Bwsl[sa + 1]])
                for j in range(2):
                    fc = 2 * g + j
                    bu = (fc % 2) * 2
                    for k, sl in ((0, sa), (1, sa + 1)):
                        def f(e, k=k, sl=sl, bu=bu, j=j):
                            ins = None
                            for ch in range(DC):
                                ins = e.matmul(banks[bu + k][:, 0:N], c.wsl[sl][:, ch, j * 128:(j + 1) * 128],
                                               c.hT[:, ch, 0:N], start=(ch == 0), stop=(ch == DC - 1))
                            return ins
                        fw.op("pe", f, reads=c.BhT + [c.Bwsl[sl]], writes=[PB[bu + k]])
                    sgi = fc % 2
                    fw.op("act", lambda e, bu=bu, sgi=sgi: e.activation(out=c.sg[sgi][:, 0:N], in_=banks[bu][:, 0:N],
                                                                        func=AF.Silu),
                          reads=[PB[bu]], writes=[c.Bsg[sgi]])
                    fw.op("dve", lambda e, bu=bu, sgi=sgi, fc=fc: e.tensor_tensor(
                        out=c.gT[:, fc, 0:N], in0=c.sg[sgi][:, 0:N], in1=banks[bu + 1][:, 0:N], op=ALU.mult),
                        reads=[c.Bsg[sgi], PB[bu + 1]], writes=[c.BgT[fc]])
            for dc in range(DC):
                sl = dc % 2
                fw.dma("sp", c.Sw2sl[sl], c.w2sl[sl][:], w2s[dc], reads=list(Bw2[dc]) if isinstance(Bw2[dc], tuple) else [Bw2[dc]], writes=[c.Bw2sl[sl]])

                def f(e, sl=sl, dc=dc):
                    ins = None
                    for fc in range(FC):
                        ins = e.matmul(banks[4 + dc % 2][:, 0:N], c.w2sl[sl][:, fc, :], c.gT[:, fc, 0:N],
                                       start=(fc == 0), stop=(fc == FC - 1))
                    return ins
                fw.op("pe", f, reads=c.BgT + [c.Bw2sl[sl]], writes=[PB[4 + dc % 2]])
                fw.op("dve", lambda e, dc=dc: e.scalar_tensor_tensor(
                    out=c.xT[:, dc, 0:N], in0=banks[4 + dc % 2][:, 0:N], scalar=0.5, in1=c.xT[:, dc, 0:N],
                    op0=ALU.mult, op1=ALU.add),
                    reads=[PB[4 + dc % 2], c.BxT[dc]], writes=[c.BxT[dc]])

        def alloc_common(ph, c):
            def sb(name, shape, dt):
                return ph.enter_context(nc.sbuf_tensor(c.tag + "_" + name, list(shape), dt))
            c.sb = sb
            c.xT = sb("xT", [128, DC, 512], F32); c.BxT = [Buf() for _ in range(DC)]
            c.hT = sb("hT", [128, DC, 512], BF16); c.BhT = [Buf() for _ in range(DC)]
            bigsz = max(FC * 512, 16384)
            c.big = sb("big", [128, bigsz], BF16)
            c.gT = c.big[:, 0:FC * 512].rearrange("p (f n) -> p f n", n=512)
            c.BgT = [Buf() for _ in range(FC)]
            c.wsl = [sb(f"wsl{i}", [128, DC, 256], BF16) for i in range(4)]
            c.Bwsl = [Buf() for _ in range(4)]; c.Swsl = [fw.dsem(f"{c.tag}wsl{i}") for i in range(4)]
            c.w2sl = [sb(f"w2sl{i}", [128, FC, 128], BF16) for i in range(2)]
            c.Bw2sl = [Buf() for _ in range(2)]; c.Sw2sl = [fw.dsem(f"{c.tag}w2sl{i}") for i in range(2)]
            c.sg = [sb(f"sg{i}", [128, 512], BF16) for i in range(2)]; c.Bsg = [Buf(), Buf()]
            c.rstd = sb("rstd", [128, 512], F32); c.Brstd = Buf()

        def phaseA():
            with contextlib.ExitStack() as ph:
                c = Ctx(); c.tag = "A"
                alloc_common(ph, c)
                sb = c.sb
                xs0 = sb("xs", [128, D], F32)
                stage32 = c.big[:, 0:16384].bitcast(F32)
                if FC * 512 >= 16384 + 2 * D:
                    xs1 = c.big[:, 16384:16384 + 2 * D].bitcast(F32)
                    xsl = [xs0[:], xs1]
                else:
                    xsl = [xs0[:], xs0[:]]
                nxs = 2 if FC * 512 >= 16384 + 2 * D else 1
                Bxsl = [Buf() for _ in range(nxs)]; Sxsl = [fw.dsem(f"Sxs{i}") for i in range(nxs)]

                def stage(par):
                    return stage32[:, par * 4096:(par + 1) * 4096].rearrange("p (t e) -> p t e", e=1024)
                Bstage = [Buf(), Buf()]
                ropet = sb("ropet", [128, 4, 32], F32); Bropet = Buf(); Sropet = fw.dsem("Sropet")
                junk = sb("junk", [128, 128], BF16); Bjunk = Buf()
                ssq = [sb(f"ssq{i}", [128, 2], F32) for i in range(6)]; Bssq = [Buf() for _ in range(6)]
                rs2 = [sb(f"rs2{i}", [128, 2], F32) for i in range(6)]; Brs2 = [Buf() for _ in range(6)]
                b16 = [sb(f"b16_{i}", [128, 1024], BF16) for i in range(2)]; Bb16 = [Buf(), Buf()]
                Tst = [sb(f"Tst{i}", [128, 8, 512], BF16) for i in range(2)]; BTst = [Buf(), Buf()]
                rt = [sb(f"rt{i}", [128, 8, 16], F32) for i in range(4)]; Brt = Buf()
                zt = sb("zt", [128, 8], F32); Bzt = Buf()
                Sst = [fw.dsem(f"Sst{i}") for i in range(6)]
                Sx1 = fw.dsem("Sx1"); Slf = fw.dsem("Slf")
                types = [("qa", 0, o["gqa"], True), ("ka", 4, o["gka"], True), ("va", 8, None, False),
                         ("qb", 12, o["gqb"], False), ("kb", 16, o["gkb"], False), ("vb", 20, None, False)]
                tcount = [0]
                gcount = [0]
                qcount = [0]
                pending = []

                for b in range(NBLK + 1):
                    misc = (b == NBLK)
                    nt = 3 if misc else 4
                    N = nt * 128
                    own = misc or (b % 2 == 0)
                    oi = NPAIR if misc else b // 2
                    row0 = b * 512
                    orow0 = oi * 512
                    for sbuf in Bstage:
                        for gb in c.BgT:
                            if sbuf.w is not None:
                                gb.r[("w", id(sbuf))] = sbuf.w
                            for k2, tk in sbuf.r.items():
                                gb.r[(k2, id(sbuf))] = tk
                    fw.dma("sp", Sropet, ropet[:, 0:nt, :],
                           rope_in[row0:row0 + N, :].rearrange("(t p) e -> p t e", p=128), writes=[Bropet])
                    for ts in range(nt):
                        xi = ts % nxs
                        xs = xsl[xi]; Bxs = Bxsl[xi]
                        fw.dma("sp", Sxsl[xi], xs, xin[row0 + ts * 128: row0 + (ts + 1) * 128, :],
                               writes=[Bxs] + (c.BgT if xi == 1 else []))
                        for q4 in range((DC + 3) // 4):
                            nch = min(4, DC - q4 * 4)

                            def f(e, q4=q4, nch=nch, xs=None):
                                ins = None
                                for k in range(nch):
                                    ch = q4 * 4 + k
                                    ins = e.transpose(banks[q4 % 4][:, k * 128:(k + 1) * 128],
                                                      xs[:, ch * 128:(ch + 1) * 128], ident_f[:])
                                return ins
                            fw.op("pe", lambda e, f=f, xs=xs: f(e, xs=xs), reads=[Bxs, BC], writes=[PB[q4 % 4]])
                            eng = "act" if q4 % 2 == 0 else "dve"
                            if eng == "act":
                                fn = lambda e, q4=q4, nch=nch, ts=ts: e.activation(
                                    out=c.xT[:, q4 * 4:q4 * 4 + nch, ts * 128:(ts + 1) * 128],
                                    in_=banks[q4 % 4][:, 0:nch * 128].rearrange("p (c n) -> p c n", n=128), func=AF.Copy)
                            else:
                                fn = lambda e, q4=q4, nch=nch, ts=ts: e.tensor_copy(
                                    out=c.xT[:, q4 * 4:q4 * 4 + nch, ts * 128:(ts + 1) * 128],
                                    in_=banks[q4 % 4][:, 0:nch * 128].rearrange("p (c n) -> p c n", n=128))
                            fw.op(eng, fn, reads=[PB[q4 % 4]], writes=c.BxT[q4 * 4:q4 * 4 + nch])
                    rmsnorm_fm(c, N, 0)
                    ffn(c, N, w1a_s, w3a_s, w2a_s, Bw1a, Bw3a, Bw2a)
                    if own:
                        fw.dma("act", Sx1, x1_s[oi][:, :, 0:N], c.xT[:, :, 0:N], reads=c.BxT)
                    rmsnorm_fm(c, N, DC)
                    for sbuf in Bstage:
                        for gb in c.BgT:
                            for k2, tk in gb.r.items():
                                sbuf.r[(k2, id(gb))] = tk
                            if gb.w is not None:
                                sbuf.r[("w", id(gb))] = gb.w
                    for (tname, g0, goff, dorope) in types:
                        isq = tname[0] == "q"
                        isv = tname[0] == "v"
                        if isq and not own:
                            continue
                        par = tcount[0] % 2
                        tcount[0] += 1
                        st = stage(par)
                        Bst = Bstage[par]
                        for gi in range(4):
                            g = g0 + gi
                            sl = gcount[0] % 4
                            gcount[0] += 1
                            fw.dma("sp", c.Swsl[sl], c.wsl[sl][:], win_s[g], reads=[Bwin[g]], writes=[c.Bwsl[sl]])
                            for ts in range(nt):
                                bk = qcount[0] % 6
                                qcount[0] += 1

                                def f(e, sl=sl, ts=ts, bk=bk):
                                    ins = None
                                    for ch in range(DC):
                                        ins = e.matmul(banks[bk][:, 0:256], c.hT[:, ch, ts * 128:(ts + 1) * 128],
                                                       c.wsl[sl][:, ch, :], start=(ch == 0), stop=(ch == DC - 1))
                                    return ins
                                fw.op("pe", f, reads=c.BhT + [c.Bwsl[sl]], writes=[PB[bk]])
                                if isv:
                                    fw.op("act", lambda e, ts=ts, gi=gi, bk=bk, st=st: e.activation(
                                        out=st[:, ts, gi * 256:(gi + 1) * 256], in_=banks[bk][:, 0:256], func=AF.Copy),
                                        reads=[PB[bk]], writes=[Bst])
                                else:
                                    for hm in range(2):
                                        fw.op("act", lambda e, hm=hm, bk=bk: e.activation(
                                            out=junk[:], in_=banks[bk][:, hm * 128:(hm + 1) * 128], func=AF.Square,
                                            accum_out=ssq[bk][:, hm:hm + 1]),
                                            reads=[PB[bk]], writes=[Bjunk, Bssq[bk]])
                                    fw.op("act", lambda e, bk=bk: e.activation(
                                        out=rs2[bk][:], in_=ssq[bk][:], func=AF.Sqrt, bias=epsD[:, 1:2], scale=1.0),
                                        reads=[Bssq[bk], BC], writes=[Brs2[bk]])
                                    fw.op("dve", lambda e, bk=bk: e.reciprocal(out=rs2[bk][:], in_=rs2[bk][:]),
                                          reads=[Brs2[bk]], writes=[Brs2[bk]])
                                    for hm in range(2):
                                        fw.op("dve", lambda e, hm=hm, bk=bk, ts=ts, gi=gi, st=st, goff=goff: e.scalar_tensor_tensor(
                                            out=st[:, ts, gi * 256 + hm * 128: gi * 256 + (hm + 1) * 128],
                                            in0=banks[bk][:, hm * 128:(hm + 1) * 128], scalar=rs2[bk][:, hm:hm + 1],
                                            in1=hv[:, goff:goff + 128], op0=ALU.mult, op1=ALU.mult),
                                            reads=[PB[bk], Brs2[bk], BC], writes=[Bst])
                        while pending:
                            pending.pop(0)()
                        tpar = par
                        for ts in range(nt):
                            if dorope:
                                S3 = st[:, ts, :].rearrange("p (h d) -> p h d", d=128)
                                x1 = S3[:, :, 0:16]; x2 = S3[:, :, 16:32]
                                cosb = ropet[:, ts, 0:16].unsqueeze(1).broadcast_to([128, 8, 16])
                                sinb = ropet[:, ts, 16:32].unsqueeze(1).broadcast_to([128, 8, 16])
                                rd = [Bst, Bropet]
                                fw.op("dve", lambda e, x1=x1, cosb=cosb: e.tensor_tensor(out=rt[0][:], in0=x1, in1=cosb, op=ALU.mult), reads=rd, writes=[Brt])
                                fw.op("dve", lambda e, x2=x2, sinb=sinb: e.tensor_tensor(out=rt[1][:], in0=x2, in1=sinb, op=ALU.mult), reads=rd, writes=[Brt])
                                fw.op("dve", lambda e, x1=x1, sinb=sinb: e.tensor_tensor(out=rt[2][:], in0=x1, in1=sinb, op=ALU.mult), reads=rd, writes=[Brt])
                                fw.op("dve", lambda e, x2=x2, cosb=cosb: e.tensor_tensor(out=rt[3][:], in0=x2, in1=cosb, op=ALU.mult), reads=rd, writes=[Brt])
                                fw.op("dve", lambda e, x1=x1: e.tensor_tensor(out=x1, in0=rt[0][:], in1=rt[1][:], op=ALU.subtract), reads=[Brt], writes=[Bst])
                                fw.op("dve", lambda e, x2=x2: e.tensor_tensor(out=x2, in0=rt[2][:], in1=rt[3][:], op=ALU.add), reads=[Brt], writes=[Bst])
                            bi = ts % 2
                            fw.op("dve", lambda e, bi=bi, ts=ts, st=st: e.tensor_copy(out=b16[bi][:], in_=st[:, ts, :]),
                                  reads=[Bst], writes=[Bb16[bi]])
                            if isv:
                                c0 = 0 if tname == "va" else 1024
                                fw.dma("act", Sst[2 + bi], v_s[row0 + ts * 128: row0 + (ts + 1) * 128, c0:c0 + 1024],
                                       b16[bi][:], reads=[Bb16[bi]])
                            else:
                                def f(e, bi=bi):
                                    ins = None
                                    for hm in range(8):
                                        ins = e.transpose(banks[6 + bi][:].bitcast(BF16)[:, hm * 128:(hm + 1) * 128],
                                                          b16[bi][:, hm * 128:(hm + 1) * 128], ident_b[:])
                                    return ins
                                fw.op("pe", f, reads=[Bb16[bi], BC], writes=[PB[6 + bi]])
                                fw.op("dve", lambda e, bi=bi, ts=ts, tpar=tpar: e.tensor_copy(
                                    out=Tst[tpar][:, :, ts * 128:(ts + 1) * 128],
                                    in_=banks[6 + bi][:].bitcast(BF16).rearrange("p (h n) -> p h n", n=128)),
                                    reads=[PB[6 + bi]], writes=[BTst[tpar]])
                        def mk_stores(tname=tname, isq=isq, isv=isv, own=own, par=par, tpar=tpar, st=st, Bst=Bst,
                                      orow0=orow0, row0=row0, N=N, nt=nt):
                            if (not isq) and own:
                                dst = {"ka": ak_out, "va": av_out, "kb": bk_out, "vb": bv_out}[tname]
                                fw.dma("act", Sst[par], dst[orow0:orow0 + N, :].rearrange("(t p) e -> p t e", p=128),
                                       st[:, 0:nt, :], reads=[Bst])
                            if not isv:
                                hm0 = 0 if tname[1] == "a" else 8
                                if isq:
                                    fw.dma("act", Sst[4 + tpar],
                                           qT_s[hm0:hm0 + 8, :, orow0:orow0 + N].rearrange("h d n -> d h n"),
                                           Tst[tpar][:, :, 0:N], reads=[BTst[tpar]])
                                else:
                                    fw.dma("act", Sst[4 + tpar],
                                           kT_s[hm0:hm0 + 8, :, row0:row0 + N].rearrange("h d n -> d h n"),
                                           Tst[tpar][:, :, 0:N], reads=[BTst[tpar]])
                        pending.append(mk_stores)
                    while pending:
                        pending.pop(0)()
                    if b == 0:
                        fw.dma("sp", Swfb, wfb[:], wfb_s, reads=[Bwfb_cast], writes=[Bwfb])
                    for ts in range(nt):
                        bk = qcount[0] % 6
                        qcount[0] += 1

                        def f(e, ts=ts, bk=bk):
                            ins = None
                            for ch in range(DC):
                                ins = e.matmul(banks[bk][:, 0:8], c.hT[:, ch, ts * 128:(ts + 1) * 128],
                                               wfb[:, ch, :], start=(ch == 0), stop=(ch == DC - 1))
                            return ins
                        fw.op("pe", f, reads=c.BhT + [Bwfb], writes=[PB[bk]])
                        tile_i = b * 4 + ts
                        fw.op("dve", lambda e, bk=bk: e.tensor_tensor(out=zt[:], in0=banks[bk][:, 0:8],
                                                                      in1=hv[:, o["bf"]:o["bf"] + 8], op=ALU.add),
                              reads=[PB[bk], BC], writes=[Bzt])
                        fw.op("act", lambda e: e.activation(out=zt[:], in_=zt[:], func=AF.Exp, scale=-1.0),
                              reads=[Bzt], writes=[Bzt])
                        fw.op("act", lambda e: e.activation(out=zt[:], in_=zt[:], func=AF.Ln, bias=epsD[0:128, 3:4], scale=1.0),
                              reads=[Bzt], writes=[Bzt])
                        fw.op("dve", lambda e, tile_i=tile_i: e.tensor_scalar(out=lf_all[:, tile_i, :], in0=zt[:], scalar1=-1.0,
                                                                              scalar2=None, op0=ALU.mult),
                              reads=[Bzt], writes=[Blf])
                    if own:
                        fw.dma("act", Slf, lf_out[orow0:orow0 + N, :].rearrange("(t p) h -> p t h", p=128),
                               lf_all[:, b * 4:b * 4 + nt, :], reads=[Blf])
                fw.barrier()

        def phaseB():
            with contextlib.ExitStack() as ph:
                def sb(name, shape, dt):
                    return ph.enter_context(nc.sbuf_tensor("B_" + name, list(shape), dt))
                MT = NKT + 2
                tot_all = sb("tot_all", [128, NKT, 8], F32); cum_all = sb("cum_all", [128, NKT, 8], F32)
                negc = sb("negc", [128, NKT + 1, 8], F32)
                cref = sb("cref", [128, NPAIR, 8], F32)
                carry = sb("carry", [128, 8], F32); mtot = sb("mtot", [128, 8], F32)
                tb0 = sb("tb0", [128, 8], F32); tb1 = sb("tb1", [128, 8], F32)
                rb = sb("rb", [128, 8], F32); metab = sb("metab", [128, 8], F32)
                clf_t = sb("clf_t", [128, 4, PKT, 8], F32); Sclf = fw.dsem("Sclf")
                sbias_c = sb("sbias_c", [128, 4 * PKT, 8], F32); sbias_n = sb("sbias_n", [128, 4, 8], F32)
                zcol = sb("zcol", [128, 1], F32)
                Bcs = Buf()
                fw.dma("sp", Sclf, clf_t[:], clf_in.rearrange("b (t p) h -> p b t h", p=128), writes=[Bcs])
                fw.op("dve", lambda e: e.memset(zcol[:], 0.0), writes=[Bcs])
                zpad = sb("zpad", [128, 16, 112], BF16); Bzpad = Buf(); Szpad = fw.dsem("Szpad")
                fw.op("dve", lambda e: e.memset(zpad[:], 0.0), writes=[Bzpad])
                fw.dma("act", Szpad, oT_s[:, :, NQ - 112:NQ].rearrange("c d n -> d c n"), zpad[:], reads=[Bzpad])
                fw.op("dve", lambda e: e.memset(negc[:], 0.0), writes=[Bcs])

                def f(e):
                    ins = None
                    for j in range(NKT):
                        ins = e.matmul(banks[0][:, j * 8:(j + 1) * 8], ones_f[:], lf_all[:, j, :], start=True, stop=True)
                    return ins
                fw.op("pe", f, reads=[Blf, BC], writes=[PB[0]])

                def f(e):
                    ins = None
                    for j in range(NKT):
                        ins = e.matmul(banks[1][:, j * 8:(j + 1) * 8], umat[:], lf_all[:, j, :], start=True, stop=True)
                    return ins
                fw.op("pe", f, reads=[Blf, BC], writes=[PB[1]])

                def f(e):
                    e.matmul(banks[2][:, 0:8], ones_f[0:16, :], lf_all[0:16, MT, :], start=True, stop=True)
                    return e.matmul(banks[2][0:16, 8:16], umat[0:16, 0:16], lf_all[0:16, MT, :], start=True, stop=True)
                fw.op("pe", f, reads=[Blf, BC], writes=[PB[2]])
                fw.op("dve", lambda e: e.tensor_copy(out=tot_all[:], in_=banks[0][:, 0:NKT * 8].rearrange("p (t h) -> p t h", h=8)),
                      reads=[PB[0]], writes=[Bcs])
                fw.op("dve", lambda e: e.tensor_copy(out=cum_all[:], in_=banks[1][:, 0:NKT * 8].rearrange("p (t h) -> p t h", h=8)),
                      reads=[PB[1]], writes=[Bcs])
                fw.op("dve", lambda e: e.tensor_copy(out=carry[:], in_=banks[2][:, 0:8]), reads=[PB[2]], writes=[Bcs])
                fw.op("dve", lambda e: e.tensor_copy(out=mtot[:], in_=banks[2][:, 0:8]), reads=[PB[2]], writes=[Bcs])
                fw.op("dve", lambda e: e.tensor_scalar(out=negc[0:16, NKT, :], in0=banks[2][0:16, 8:16], scalar1=-1.0, scalar2=None,
                                                       op0=ALU.mult), reads=[PB[2]], writes=[Bcs])
                fw.op("dve", lambda e: e.tensor_tensor(out=metab[0:16, :], in0=negc[0:16, NKT, :], in1=mtot[0:16, :], op=ALU.add),
                      reads=[Bcs], writes=[Bcs])

                def dv(fn):
                    fw.op("dve", fn, reads=[Bcs, BC], writes=[Bcs])
                for i in range(NPAIR):
                    t0 = 8 * i
                    dv(lambda e, t0=t0: e.tensor_tensor(out=tb0[:], in0=tot_all[:, t0, :], in1=tot_all[:, t0 + 1, :], op=ALU.add))
                    dv(lambda e, t0=t0: e.tensor_tensor(out=tb0[:], in0=tb0[:], in1=tot_all[:, t0 + 2, :], op=ALU.add))
                    dv(lambda e, t0=t0: e.tensor_tensor(out=tb0[:], in0=tb0[:], in1=tot_all[:, t0 + 3, :], op=ALU.add))
                    dv(lambda e, t0=t0: e.tensor_tensor(out=tb1[:], in0=tot_all[:, t0 + 4, :], in1=tot_all[:, t0 + 5, :], op=ALU.add))
                    dv(lambda e, t0=t0: e.tensor_tensor(out=tb1[:], in0=tb1[:], in1=tot_all[:, t0 + 6, :], op=ALU.add))
                    dv(lambda e, t0=t0: e.tensor_tensor(out=tb1[:], in0=tb1[:], in1=tot_all[:, t0 + 7, :], op=ALU.add))
                    for blk in range(2):
                        if blk == 0:
                            dv(lambda e: e.scalar_tensor_tensor(out=rb[:], in0=tb1[:], scalar=pm[:, 0:1], in1=carry[:],
                                                                op0=ALU.mult, op1=ALU.add))
                        else:
                            dv(lambda e: e.scalar_tensor_tensor(out=rb[:], in0=tb0[:], scalar=pm[:, 2:3], in1=carry[:],
                                                                op0=ALU.mult, op1=ALU.add))
                        for s4 in range(4):
                            tl = t0 + blk * 4 + s4
                            if blk == 0 and s4 == 2:
                                dv(lambda e, i=i: e.tensor_copy(out=cref[:, i, :], in_=rb[:]))
                            dv(lambda e, tl=tl: e.scalar_tensor_tensor(out=negc[:, tl, :], in0=cum_all[:, tl, :], scalar=-1.0,
                                                                       in1=rb[:], op0=ALU.mult, op1=ALU.subtract))
                            if s4 < 3:
                                dv(lambda e, tl=tl: e.tensor_tensor(out=rb[:], in0=rb[:], in1=tot_all[:, tl, :], op=ALU.add))
                    dv(lambda e: e.tensor_tensor(out=carry[:], in0=carry[:], in1=tb0[:], op=ALU.add))
                    dv(lambda e: e.tensor_tensor(out=carry[:], in0=carry[:], in1=tb1[:], op=ALU.add))

                KDBG = int(os.environ.get("KDBG", "99"))
                if KDBG <= 1:
                    fw.barrier()
                    return
                def f(e):
                    ins = None
                    for b in range(4):
                        pb = 64 * (b % 2)
                        nt_ = NKT + b // 2
                        for kt in range(PKT):
                            col = (b * PKT + kt) * 8
                            ins = e.matmul(banks[3][:, col:col + 8], lmat[:], clf_t[:, b, kt, :], start=True, stop=False)
                            for k2 in range(kt + 1, PKT):
                                ins = e.matmul(banks[3][:, col:col + 8], ones_f[:], clf_t[:, b, k2, :], start=False, stop=False)
                            hb = (b % 2) * 192
                            ins = e.matmul(banks[3][:, col:col + 8], selm[:, hb:hb + 128], lf_all[:, nt_, :],
                                           start=False, stop=True)
                    return ins
                fw.op("pe", f, reads=[Blf, BC, Bcs], writes=[PB[3]])

                def f(e):
                    ins = None
                    for b in range(4):
                        pb = 64 * (b % 2)
                        nt_ = NKT + b // 2
                        hb = (b % 2) * 192
                        ins = e.matmul(banks[4][0:64, b * 8:(b + 1) * 8], selm[:, hb + 128:hb + 192],
                                       lf_all[:, nt_, :], start=True, stop=True)
                    return ins
                fw.op("pe", f, reads=[Blf, BC], writes=[PB[4]])
                fw.op("dve", lambda e: e.tensor_copy(out=sbias_c[:], in_=banks[3][:, 0:4 * PKT * 8].rearrange("p (t h) -> p t h", h=8)),
                      reads=[PB[3]], writes=[Bcs])
                fw.op("dve", lambda e: e.tensor_copy(out=sbias_n[0:64, :, :], in_=banks[4][0:64, 0:32].rearrange("p (t h) -> p t h", h=8)),
                      reads=[PB[4]], writes=[Bcs])

                if KDBG <= 2:
                    fw.barrier()
                    return
                KTm = sb("KTm", [128, 2, SEQ], BF16); BKTm = Buf(); SKTm = fw.dsem("SKTm")
                Vm = sb("Vm", [128, NKT, 256], BF16); BVm = Buf(); SVm = fw.dsem("SVm")
                KTx = [sb(f"KTx{i}", [128, 2, 384], BF16) for i in range(2)]
                Vn = [sb(f"Vn{i}", [64, 4, 256], BF16) for i in range(2)]
                Vmeta = [sb(f"Vmeta{i}", [16, 256], BF16) for i in range(2)]
                Bmx = [Buf(), Buf()]; Smx = [fw.dsem(f"Smx{i}") for i in range(2)]
                Qt = [sb(f"Qt{i}", [128, 2, 512], BF16) for i in range(2)]; BQt = [Buf(), Buf()]
                SQt = [fw.dsem(f"SQt{i}") for i in range(2)]
                Pt = [sb(f"Pt{i}", [128, 512], BF16) for i in range(6)]; BPt = [Buf() for _ in range(6)]
                biasb = [sb(f"biasb{i}", [128, 2, NKT + 1], F32) for i in range(2)]; Bbias = [Buf(), Buf()]
                o1q = [sb(f"o1q{i}", [128, 256], F32) for i in range(4)]; Bo1q = [Buf() for _ in range(4)]
                ss8 = sb("ss8", [128, 8], F32); Bss8 = Buf()
                lsb = sb("lsb", [128, 8], F32); rl = sb("rl", [128, 8], F32); nl4 = sb("nl4", [128, 4], F32); Bl = Buf()
                o1 = [sb(f"o1_{i}", [128, 256], F32) for i in range(2)]; Bo1 = [Buf(), Buf()]
                onb = [[sb(f"on{g}_{q}", [128, 256], BF16) for q in range(4)] for g in range(4)]
                Bonb = [[Buf() for q in range(4)] for g in range(4)]
                pending_tails = []

                def step_tick():
                    for t in pending_tails:
                        t[0] += 1
                    while pending_tails and pending_tails[0][0] >= 40:
                        pending_tails.pop(0)[1]()
                ss1 = [sb(f"ss1_{i}", [128, 2], F32) for i in range(2)]; Bss1 = [Buf(), Buf()]
                sqf = sb("sqf", [128, 256], F32); Bsqf = Buf()
                oTst = [sb(f"oTst{i}", [128, 2, 512], BF16) for i in range(4)]; BoTst = [Buf() for _ in range(4)]
                SoT = [fw.dsem(f"SoT{i}") for i in range(4)]
                ckb = [sb(f"ckb{i}", [128, PKT, 256], BF16) for i in range(2)]; Bckb = [Buf(), Buf()]
                cvb = [sb(f"cvb{i}", [128, PKT, 256], BF16) for i in range(2)]; Bcvb = [Buf(), Buf()]
                Sck = [fw.dsem(f"Sck{i}") for i in range(2)]; Scv = [fw.dsem(f"Scv{i}") for i in range(2)]
                cKT = [sb(f"cKT{i}", [128, 2, PAST], BF16) for i in range(2)]; BcKT = [Buf(), Buf()]
                gcnt = [0]
                tcnt = [0]
                scnt = [0]
                ecnt = [0]
                mA = lambda j: masks_b[:, j * 512:(j + 1) * 512]
                mB = lambda j: masks_b[:, 2048 + j * 512: 2048 + (j + 1) * 512]
                m_meta = masks_b[0:16, 4096:4112]
                m_new = masks_b[0:64, 4112:4176]

                def attend(u, nq, qcol0, tiles, extra_reads):
                    diff = u < 4
                    vw = 256 if diff else 128
                    gpar = gcnt[0] % 2
                    gcnt[0] += 1
                    fw.dma("sp", SQt[gpar], Qt[gpar][:, :, 0:nq],
                           qT_s[2 * u:2 * u + 2, :, qcol0:qcol0 + nq].rearrange("m d n -> d m n"), writes=[BQt[gpar]])
                    nqt = (nq + 127) // 128
                    steps = [(ti, m) for ti in range(len(tiles)) for m in range(2)]

                    def acc(m, qs, nqs):
                        return banks[m * 2 + qs // 2][0:nqs, (qs % 2) * 256:(qs % 2) * 256 + vw]

                    def qk(n):
                        ti, m = steps[n]
                        kT_fn, V_fn, nk, mask_fn, bias_fn = tiles[ti]
                        bk = 5 + scnt[0] % 3
                        mk = mask_fn(m)

                        def f(e):
                            ins = e.matmul(banks[bk][0:nk, 0:nq], kT_fn(m), Qt[gpar][:, m, 0:nq], start=True, stop=(mk is None))
                            if mk is not None:
                                ins = e.matmul(banks[bk][0:nk, 0:nq], ident_b[0:nk, 0:nk], mk, start=False, stop=True)
                            return ins
                        fw.op("pe", f, reads=[BQt[gpar], BC] + extra_reads, writes=[PB[bk]])
                        r = (bk, scnt[0] % 6)
                        scnt[0] += 1
                        return r

                    def ex(n, bk, ps):
                        ti, m = steps[n]
                        kT_fn, V_fn, nk, mask_fn, bias_fn = tiles[ti]
                        fw.op("act", lambda e: e.activation(out=Pt[ps][0:nk, 0:nq], in_=banks[bk][0:nk, 0:nq], func=AF.Exp,
                                                            bias=bias_fn(m), scale=SCALE),
                              reads=[PB[bk], Bbias[0], Bbias[1], Bcs, BC], writes=[BPt[ps]])

                    def pv(n, ps):
                        ti, m = steps[n]
                        kT_fn, V_fn, nk, mask_fn, bias_fn = tiles[ti]
                        first = (ti == 0)
                        last = (ti == len(tiles) - 1)

                        def f(e):
                            ins = None
                            for qs in range(nqt):
                                nqs = min(128, nq - qs * 128)
                                lhs = Pt[ps][0:nk, qs * 128:qs * 128 + nqs]
                                e.matmul(acc(m, qs, nqs), lhs, V_fn(m), start=False, stop=False)
                                ins = e.matmul(banks[4][0:nqs, m * 4 + qs:m * 4 + qs + 1], lhs, ones_b[0:nk, 0:1],
                                               start=False, stop=False)
                            return ins
                        fw.op("pe", f, reads=[BPt[ps], BC] + extra_reads, writes=[PB[2 * m], PB[2 * m + 1], PB[4]])

                    for t_ in pending_tails:
                        t_[0] += 1
                    while pending_tails and pending_tails[0][0] >= 2:
                        pending_tails.pop(0)[1]()
                    def fz(e):
                        ins = None
                        for bkz in ([0, 1, 2, 3, 4] if nqt > 2 else [0, 2, 4]):
                            ins = e.matmul(banks[bkz][:, :], zb[0:1, 0:128], zb[0:1, 0:512], start=True, stop=False)
                        return ins
                    fw.op("pe", fz, reads=[BC], writes=[PB[0], PB[1], PB[2], PB[3], PB[4]])

                    def fclose(e):
                        ins = None
                        for bkz in ([0, 1, 2, 3, 4] if nqt > 2 else [0, 2, 4]):
                            ins = e.matmul(banks[bkz][:, :], zb[0:1, 0:128], zb[0:1, 0:512], start=False, stop=True)
                        return ins
                    LA = 2
                    pend = [qk(n) for n in range(min(LA, len(steps)))]
                    for n in range(len(steps)):
                        cur = pend.pop(0)
                        if n + LA < len(steps):
                            pend.append(qk(n + LA))
                        ex(n, cur[0], cur[1])
                        pv(n, cur[1])
                    fw.op("pe", fclose, reads=[BC], writes=[PB[0], PB[1], PB[2], PB[3], PB[4]])
                    while len(pending_tails) >= 3:
                        pending_tails.pop(0)[1]()
                    R = min(128, nq)
                    fw.op("dve", lambda e: e.tensor_scalar(out=lsb[0:R, :], in0=banks[4][0:R, 0:8], scalar1=1e-37, scalar2=None, op0=ALU.max),
                          reads=[PB[4]], writes=[Bl])
                    fw.op("dve", lambda e: e.reciprocal(out=rl[0:R, :], in_=lsb[0:R, :]), reads=[Bl], writes=[Bl])
                    if diff:
                        fw.op("dve", lambda e: e.tensor_scalar(out=nl4[0:R, :], in0=rl[0:R, 4:8], scalar1=lamt[0:R, 1:2], scalar2=None,
                                                               op0=ALU.mult), reads=[Bl, BC], writes=[Bl])
                    oset = tcnt[0] % 4
                    tcnt[0] += 1
                    opar = oset
                    for qs in range(nqt):
                        nqs = min(128, nq - qs * 128)
                        if diff:
                            fw.op("dve", lambda e, qs=qs, nqs=nqs: e.tensor_scalar(
                                out=o1q[qs][0:nqs, :], in0=acc(0, qs, nqs), scalar1=rl[0:nqs, qs:qs + 1], scalar2=None,
                                op0=ALU.mult), reads=[PB[qs // 2], Bl], writes=[Bo1q[qs]])
                            fw.op("dve", lambda e, qs=qs, nqs=nqs: e.scalar_tensor_tensor(
                                out=o1q[qs][0:nqs, :], in0=acc(1, qs, nqs), scalar=nl4[0:nqs, qs:qs + 1], in1=o1q[qs][0:nqs, :],
                                op0=ALU.mult, op1=ALU.add), reads=[PB[2 + qs // 2], Bl, Bo1q[qs]], writes=[Bo1q[qs]])
                        else:
                            for hd in range(2):
                                fw.op("dve", lambda e, qs=qs, nqs=nqs, hd=hd: e.tensor_scalar(
                                    out=o1q[qs][0:nqs, hd * 128:(hd + 1) * 128], in0=acc(hd, qs, nqs),
                                    scalar1=rl[0:nqs, hd * 4 + qs:hd * 4 + qs + 1], scalar2=None, op0=ALU.mult),
                                    reads=[PB[hd * 2 + qs // 2], Bl], writes=[Bo1q[qs]])
                    for qs in range(nqt):
                        nqs = min(128, nq - qs * 128)
                        fw.op("dve", lambda e, qs=qs, nqs=nqs: e.tensor_tensor(
                            out=sqf[0:nqs, :], in0=o1q[qs][0:nqs, :], in1=o1q[qs][0:nqs, :], op=ALU.mult),
                            reads=[Bo1q[qs]], writes=[Bsqf])
                        if diff:
                            fw.op("dve", lambda e, qs=qs, nqs=nqs: e.tensor_reduce(
                                out=ss8[0:nqs, qs:qs + 1], in_=sqf[0:nqs, :], axis=AX.X, op=ALU.add),
                                reads=[Bsqf], writes=[Bss8])
                        else:
                            fw.op("dve", lambda e, qs=qs, nqs=nqs: e.tensor_reduce(
                                out=ss8[0:nqs, 2 * qs:2 * qs + 2], in_=sqf[0:nqs, :].rearrange("p (h d) -> p h d", d=128),
                                axis=AX.X, op=ALU.add), reads=[Bsqf], writes=[Bss8])
                    ncs = nqt if diff else 2 * nqt
                    ecol = 2 if diff else 1
                    fw.op("act", lambda e, ncs=ncs, ecol=ecol: e.activation(
                        out=ss8[0:R, 0:ncs], in_=ss8[0:R, 0:ncs], func=AF.Sqrt, bias=epsD[0:R, ecol:ecol + 1], scale=1.0),
                        reads=[Bss8, BC], writes=[Bss8])
                    fw.op("dve", lambda e, ncs=ncs: e.reciprocal(out=ss8[0:R, 0:ncs], in_=ss8[0:R, 0:ncs]),
                          reads=[Bss8], writes=[Bss8])
                    for qs in range(nqt):
                        nqs = min(128, nq - qs * 128)
                        on_q = onb[oset][qs]; Bon_q = Bonb[oset][qs]
                        if diff:
                            fw.op("dve", lambda e, qs=qs, nqs=nqs, on_q=on_q: e.scalar_tensor_tensor(
                                out=on_q[0:nqs, :], in0=o1q[qs][0:nqs, :], scalar=ss8[0:nqs, qs:qs + 1],
                                in1=hv[0:nqs, o["goa"]:o["goa"] + 256], op0=ALU.mult, op1=ALU.mult),
                                reads=[Bo1q[qs], Bss8, BC], writes=[Bon_q])
                        else:
                            for hd in range(2):
                                fw.op("dve", lambda e, qs=qs, nqs=nqs, hd=hd, on_q=on_q: e.scalar_tensor_tensor(
                                    out=on_q[0:nqs, hd * 128:(hd + 1) * 128], in0=o1q[qs][0:nqs, hd * 128:(hd + 1) * 128],
                                    scalar=ss8[0:nqs, 2 * qs + hd:2 * qs + hd + 1], in1=hv[0:nqs, o["gob"]:o["gob"] + 128],
                                    op0=ALU.mult, op1=ALU.mult),
                                    reads=[Bo1q[qs], Bss8, BC], writes=[Bon_q])

                    def tail(u=u, nq=nq, nqt=nqt, oset=oset, opar=opar, qcol0=qcol0):
                        b7 = banks[7][:].bitcast(BF16)
                        for qs in range(nqt):
                            nqs = min(128, nq - qs * 128)
                            on_q = onb[oset][qs]; Bon_q = Bonb[oset][qs]

                            def f(e, nqs=nqs, on_q=on_q):
                                ins = None
                                for ch in range(2):
                                    ins = e.transpose(b7[:, ch * 128:ch * 128 + nqs], on_q[0:nqs, ch * 128:(ch + 1) * 128],
                                                      ident_b[0:nqs, 0:nqs])
                                return ins
                            fw.op("pe", f, reads=[Bon_q, BC], writes=[PB[7]])
                            fw.op("dve", lambda e, qs=qs, nqs=nqs: e.tensor_copy(
                                out=oTst[opar][:, :, qs * 128:qs * 128 + nqs],
                                in_=b7[:, 0:256].rearrange("p (c n) -> p c n", n=128)[:, :, 0:nqs]),
                                reads=[PB[7]], writes=[BoTst[opar]])
                        fw.dma("act", SoT[opar], oT_s[2 * u:2 * u + 2, :, qcol0:qcol0 + nq].rearrange("c d n -> d c n"),
                               oTst[opar][:, :, 0:nq], reads=[BoTst[opar]])
                    pending_tails.append([0, tail])

                def cache_load(u_, b_):
                    cb_ = b_ % 2
                    fw.dma("pool", Sck[cb_], ckb[cb_][:], ck_in[b_][:, 256 * u_:256 * u_ + 256].rearrange("(t p) e -> p t e", p=128),
                           writes=[Bckb[cb_]])
                    fw.dma("pool", Scv[cb_], cvb[cb_][:], cv_in[b_][:, 256 * u_:256 * u_ + 256].rearrange("(t p) e -> p t e", p=128),
                           writes=[Bcvb[cb_]])
                cache_load(0, 0)
                cache_load(0, 1)
                for u in range(8):
                    diff = u < 4
                    up = u % 2
                    vw = 256 if diff else 128
                    fw.dma("sp", SKTm, KTm[:], kT_s[2 * u:2 * u + 2, :, 0:SEQ].rearrange("m d n -> d m n"), writes=[BKTm])
                    for q4 in range(0, NKT, 16):
                        n4 = min(16, NKT - q4)
                        fw.dma("sp", SVm, Vm[:, q4:q4 + n4, :],
                               v_s[q4 * 128:(q4 + n4) * 128, 256 * u:256 * u + 256].rearrange("(t p) e -> p t e", p=128), writes=[BVm])
                    fw.dma("sp", Smx[up], KTx[up][:], kT_s[2 * u:2 * u + 2, :, SEQ:SEQ + 384].rearrange("m d n -> d m n"), writes=[Bmx[up]])
                    fw.dma("sp", Smx[up], Vn[up][:], v_s[SEQ:SEQ + 256, 256 * u:256 * u + 256].rearrange("(b k) e -> k b e", k=64),
                           writes=[Bmx[up]])
                    fw.dma("sp", Smx[up], Vmeta[up][:], v_s[SEQ + 256:SEQ + 272, 256 * u:256 * u + 256], writes=[Bmx[up]])
                    vsl = (lambda m: slice(0, 256)) if diff else (lambda m: slice(m * 128, (m + 1) * 128))
                    meta_tile_d = (lambda m, up=up: KTx[up][:, m, 256:272],
                                   lambda m, up=up, vsl=vsl: Vmeta[up][0:16, vsl(m)], 16)
                    for i in range(NPAIR):
                        nti = 8 * i + 8
                        bp = i % 2
                        if not diff:
                            for hd in range(2):
                                head = 2 * (u - 4) + hd
                                fw.op("dve", lambda e, hd=hd, head=head, bp=bp, nti=nti, i=i: e.tensor_scalar(
                                    out=biasb[bp][:, hd, 0:nti], in0=negc[:, 0:nti, head], scalar1=cref[:, i, head:head + 1],
                                    scalar2=None, op0=ALU.add), reads=[Bcs], writes=[Bbias[bp]])
                                fw.op("dve", lambda e, hd=hd, bp=bp, nti=nti: e.tensor_scalar(
                                    out=biasb[bp][:, hd, nti - 4:nti], in0=biasb[bp][:, hd, nti - 4:nti], scalar1=pm[:, 1:2],
                                    scalar2=None, op0=ALU.add), reads=[Bbias[bp], BC], writes=[Bbias[bp]])
                                fw.op("dve", lambda e, hd=hd, head=head, bp=bp, i=i: e.tensor_scalar(
                                    out=biasb[bp][0:16, hd, NKT:NKT + 1], in0=negc[0:16, NKT, head:head + 1],
                                    scalar1=cref[0:16, i, head:head + 1], scalar2=None, op0=ALU.add),
                                    reads=[Bcs], writes=[Bbias[bp]])
                        tiles = []
                        if diff:
                            tiles.append(meta_tile_d + (lambda m: None, lambda m: zcol[0:16, 0:1]))
                        else:
                            tiles.append(meta_tile_d + (lambda m: None, lambda m, bp=bp: biasb[bp][0:16, m, NKT:NKT + 1]))
                        for j in range(nti):
                            kf = lambda m, j=j: KTm[:, m, j * 128:(j + 1) * 128]
                            vf = lambda m, j=j, vsl=vsl: Vm[:, j, vsl(m)]
                            dj = j - 8 * i
                            if diff:
                                mf = (lambda m, dj=dj: mA(dj)) if 0 <= dj < 4 else (lambda m: None)
                                bf_ = (lambda m: pm[:, 1:2]) if dj >= 4 else (lambda m: zcol[:, 0:1])
                            else:
                                mf = (lambda m, dj=dj: mB(dj)) if 0 <= dj < 4 else (lambda m: None)
                                bf_ = lambda m, bp=bp, j=j: biasb[bp][:, m, j:j + 1]
                            tiles.append((kf, vf, 128, mf, bf_))
                        attend(u, 512, i * 512, tiles, [BKTm, BVm, Bmx[up]])
                    if KDBG <= 3:
                        break
                    if diff:
                        tiles = [meta_tile_d + (lambda m: None, lambda m: zcol[0:16, 0:1])]
                    else:
                        tiles = [meta_tile_d + (lambda m: m_meta, lambda m, u=u: metab[0:16, 2 * (u - 4) + m:2 * (u - 4) + m + 1])]
                    attend(u, 16, NPAIR * 512 + 256, tiles, [Bmx[up]])
                    if KDBG <= 4:
                        break
                    def cache_T(b, u=u):
                        cb = b % 2
                        b7 = banks[7][:].bitcast(BF16)
                        for m in range(2):
                            for k0 in range(0, PKT, 8):
                                n8 = min(8, PKT - k0)

                                def f(e, m=m, k0=k0, n8=n8, cb=cb):
                                    ins = None
                                    for k in range(n8):
                                        ins = e.transpose(b7[:, k * 128:(k + 1) * 128], ckb[cb][:, k0 + k, m * 128:(m + 1) * 128], ident_b[:])
                                    return ins
                                fw.op("pe", f, reads=[Bckb[cb], BC], writes=[PB[7]])
                                fw.op("dve", lambda e, m=m, k0=k0, n8=n8, cb=cb: e.tensor_copy(
                                    out=cKT[cb][:, m, k0 * 128:(k0 + n8) * 128], in_=b7[:, 0:n8 * 128]),
                                    reads=[PB[7]], writes=[BcKT[cb]])
                    cache_T(0)
                    for b in range(4):
                        cb = b % 2
                        if b + 1 < 4:
                            cache_T(b + 1)
                        tiles = []
                        for kt in range(PKT):
                            kf = lambda m, kt=kt, cb=cb: cKT[cb][:, m, kt * 128:(kt + 1) * 128]
                            vf = lambda m, kt=kt, cb=cb, vsl=vsl: cvb[cb][:, kt, vsl(m)]
                            if diff:
                                bf_ = lambda m: zcol[:, 0:1]
                            else:
                                bf_ = lambda m, b=b, kt=kt, u=u: sbias_c[:, b * PKT + kt, 2 * (u - 4) + m:2 * (u - 4) + m + 1]
                            tiles.append((kf, vf, 128, lambda m: None, bf_))
                        kf = lambda m, b=b, up=up: KTx[up][:, m, 64 * b:64 * b + 64]
                        vf = lambda m, b=b, up=up, vsl=vsl: Vn[up][0:64, b, vsl(m)]
                        if diff:
                            tiles.append((kf, vf, 64, lambda m: None, lambda m: zcol[0:64, 0:1]))
                        else:
                            tiles.append((kf, vf, 64, lambda m: m_new,
                                          lambda m, b=b, u=u: sbias_n[0:64, b, 2 * (u - 4) + m:2 * (u - 4) + m + 1]))
                        attend(u, 64, NPAIR * 512 + 64 * b, tiles, [BcKT[cb], Bcvb[cb], Bmx[up]])
                        if b + 2 < 4:
                            cache_load(u, b + 2)
                        elif u + 1 < 8:
                            cache_load(u + 1, b - 2)
                while pending_tails:
                    pending_tails.pop(0)[1]()
                fw.barrier()

        def phaseC():
            with contextlib.ExitStack() as ph:
                c = Ctx(); c.tag = "C"
                alloc_common(ph, c)
                sb = c.sb
                oTt = c.big[:, 0:16 * 512].rearrange("p (f n) -> p f n", n=512); BoTt = Buf(); SoTt = fw.dsem("SoTt")
                ystage = sb("ystage", [128, D], F32); Bys = Buf(); Sys = fw.dsem("Sys")
                wosl = [sb(f"wosl{i}", [128, 16, 128], BF16) for i in range(2)]; Bwosl = [Buf(), Buf()]
                Swosl = [fw.dsem(f"Swosl{i}") for i in range(2)]
                Sx = fw.dsem("Sxc")
                for oi in range(NPAIR + 1):
                    misc = (oi == NPAIR)
                    nt = 3 if misc else 4
                    N = nt * 128
                    orow0 = oi * 512
                    fw.dma("sp", Sx, c.xT[:, :, 0:N], x1_s[oi][:, :, 0:N], writes=c.BxT)
                    fw.dma("sp", SoTt, oTt[:, :, 0:N], oT_s[:, :, orow0:orow0 + N].rearrange("c d n -> d c n"),
                           writes=[BoTt] + c.BgT)
                    for dc in range(DC):
                        sl = dc % 2
                        fw.dma("sp", Swosl[sl], wosl[sl][:], wout_s[dc], reads=list(Bwout[dc]) if isinstance(Bwout[dc], tuple) else [Bwout[dc]], writes=[Bwosl[sl]])

                        def f(e, sl=sl, dc=dc):
                            ins = None
                            for fc in range(16):
                                ins = e.matmul(banks[4 + dc % 2][:, 0:N], wosl[sl][:, fc, :], oTt[:, fc, 0:N],
                                               start=(fc == 0), stop=(fc == 15))
                            return ins
                        fw.op("pe", f, reads=[BoTt, Bwosl[sl]], writes=[PB[4 + dc % 2]])
                        fw.op("dve", lambda e, dc=dc: e.tensor_tensor(out=c.xT[:, dc, 0:N], in0=banks[4 + dc % 2][:, 0:N],
                                                                      in1=c.xT[:, dc, 0:N], op=ALU.add),
                              reads=[PB[4 + dc % 2], c.BxT[dc]], writes=[c.BxT[dc]])
                    for gb in c.BgT:
                        for k2, tk in BoTt.r.items():
                            gb.r[(k2, "oTt")] = tk
                    rmsnorm_fm(c, N, 2 * DC)
                    ffn(c, N, w1b_s, w3b_s, w2b_s, Bw1b, Bw3b, Bw2b)
                    rmsnorm_fm(c, N, 3 * DC, final=True)
                    for ts in range(nt):
                        for q4 in range((DC + 3) // 4):
                            nch = min(4, DC - q4 * 4)

                            def f(e, q4=q4, nch=nch, ts=ts):
                                ins = None
                                for k in range(nch):
                                    ch = q4 * 4 + k
                                    ins = e.transpose(banks[q4 % 4][:, k * 128:(k + 1) * 128],
                                                      c.xT[:, ch, ts * 128:(ts + 1) * 128], ident_f[:])
                                return ins
                            fw.op("pe", f, reads=c.BxT[q4 * 4:q4 * 4 + nch] + [BC], writes=[PB[q4 % 4]])
                            if q4 % 2 == 0:
                                fw.op("act", lambda e, q4=q4, nch=nch: e.activation(
                                    out=ystage[:, q4 * 512:q4 * 512 + nch * 128], in_=banks[q4 % 4][:, 0:nch * 128], func=AF.Copy),
                                    reads=[PB[q4 % 4]], writes=[Bys])
                            else:
                                fw.op("dve", lambda e, q4=q4, nch=nch: e.tensor_copy(
                                    out=ystage[:, q4 * 512:q4 * 512 + nch * 128], in_=banks[q4 % 4][:, 0:nch * 128]),
                                    reads=[PB[q4 % 4]], writes=[Bys])
                        fw.dma("act", Sys, y_out[orow0 + ts * 128: orow0 + (ts + 1) * 128, :], ystage[:], reads=[Bys])
                fw.barrier()

        phaseA()
        if phases >= 2:
            phaseB()
        if phases >= 3:
            phaseC()
        fw.barrier()
    return nc


def _consts():
    ident = np.eye(128, dtype=np.float32)
    kk = np.arange(128)
    umat = (kk[:, None] <= kk[None, :]).astype(np.float32)
    lmat = (kk[:, None] > kk[None, :]).astype(np.float32)
    masks = np.zeros((128, MASK_COLS), np.float32)
    q = np.arange(512)
    for j in range(4):
        kpos = j * 128 + kk
        mA = np.where((kpos[:, None] // 64) <= (q[None, :] // 64), 0.0, NEGM)
        mB = np.where(kpos[:, None] <= q[None, :], 0.0, NEGM)
        masks[:, j * 512:(j + 1) * 512] = mA
        masks[:, 2048 + j * 512: 2048 + (j + 1) * 512] = mB
    k16 = np.arange(16)
    masks[:16, 4096:4112] = np.where(k16[:, None] <= k16[None, :], 0.0, NEGM)
    k64 = np.arange(64)
    masks[:64, 4112:4176] = np.where(k64[:, None] <= k64[None, :], 0.0, NEGM)
    sel = np.zeros((128, 384), np.float32)
    for h in range(2):
        rows = np.arange(64 * h, 64 * h + 64)
        sel[rows, h * 192: h * 192 + 128] = 1.0
        sel[rows, h * 192 + 128: h * 192 + 192] = (np.arange(64)[:, None] > np.arange(64)[None, :]).astype(np.float32)
    return ident, umat, lmat, masks, sel


def _rope_table(pos):
    half = 16
    inv = np.power(np.float32(500000.0), -np.arange(half, dtype=np.float32) * np.float32(2.0) / np.float32(32)).astype(np.float32)
    ang = pos.astype(np.float32)[:, None] * inv[None, :]
    return np.concatenate([np.cos(ang), np.sin(ang)], axis=1).astype(np.float32)


def make_in_maps(cfg, inp):
    D, SEQ, PAST, DC = cfg.D, cfg.SEQ, cfg.PAST, cfg.DC
    ident, umat, lmat, masks, sel = _consts()
    f = lambda a: np.ascontiguousarray(np.asarray(a, dtype=np.float32))
    gains = np.concatenate([f(inp[k])[0].reshape(DC, 128).T for k in ("g_ffn1", "g_mix", "g_ffn2", "g_final")], axis=1)
    gains = np.ascontiguousarray(gains)
    hvec = np.concatenate([f(inp[k])[0] for k in ("g_qa", "g_ka", "g_qb", "g_kb", "g_oa", "g_ob", "b_f",
                                                   "lambda_q1", "lambda_k1", "lambda_q2", "lambda_k2")])[None, :]
    hvec = np.ascontiguousarray(hvec)
    assert hvec.shape[1] == HV_LEN
    common = dict(ident=ident, umat=umat, lmat=lmat, masks=masks, sel=sel, gains=gains, hvec=hvec,
                  w1a=f(inp["ffn1_w1"])[0], w3a=f(inp["ffn1_w3"])[0], w2a=f(inp["ffn1_w2"])[0],
                  w1b=f(inp["ffn2_w1"])[0], w3b=f(inp["ffn2_w3"])[0], w2b=f(inp["ffn2_w2"])[0],
                  win=f(inp["w_in"])[0], wout=f(inp["w_out"])[0])
    xp = f(inp["x_prompt"]); xsm = f(inp["x_sample"]); meta = f(inp["meta_tokens"])
    cak = f(inp["cache_a_k"])[0]; cav = f(inp["cache_a_v"])[0]
    cbk = f(inp["cache_b_k"])[0]; cbv = f(inp["cache_b_v"])[0]; clf = f(inp["cache_b_logf"])[0]
    maps = []
    for core in range(8):
        s, p = core // 2, core % 2
        order = []
        for i in range(cfg.NPAIR):
            order += [2 * i + p, 2 * i + 1 - p]
        xin = np.zeros((cfg.NROWS, D), np.float32)
        pos = np.zeros((cfg.NROWS,), np.float32)
        for j, tb in enumerate(order):
            xin[j * 512:(j + 1) * 512] = xp[s, tb * 512:(tb + 1) * 512]
            pos[j * 512:(j + 1) * 512] = 16 + tb * 512 + np.arange(512)
        xin[SEQ:SEQ + 256] = xsm[4 * core:4 * core + 4].reshape(256, D)
        pos[SEQ:SEQ + 256] = np.tile(PAST + np.arange(64), 4)
        xin[SEQ + 256:SEQ + 272] = meta
        pos[SEQ + 256:SEQ + 272] = np.arange(16)
        pmv = np.zeros((128, 4), np.float32)
        pmv[:, 0] = p
        pmv[:, 1] = NEGM if p == 0 else 0.0
        pmv[:, 2] = 1 - p
        ck = np.concatenate([cak[4 * core:4 * core + 4].reshape(4, PAST, 1024),
                             cbk[4 * core:4 * core + 4].reshape(4, PAST, 1024)], axis=2)
        cv = np.concatenate([cav[4 * core:4 * core + 4].reshape(4, PAST, 1024),
                             cbv[4 * core:4 * core + 4].reshape(4, PAST, 1024)], axis=2)
        m = dict(common)
        m.update(xin=xin, rope=_rope_table(pos), pm=pmv, ck=np.ascontiguousarray(ck), cv=np.ascontiguousarray(cv),
                 clf=np.ascontiguousarray(clf[4 * core:4 * core + 4]))
        maps.append(m)
    return maps


def assemble(cfg, results, batch=4, dec_batch=32):
    D, SEQ = cfg.D, cfg.SEQ
    L = SEQ + 16
    NP = cfg.NPAIR
    y_p = np.zeros((batch, SEQ, D), np.float32)
    y_s = np.zeros((dec_batch, 64, D), np.float32)
    pk = {k: np.zeros((1, batch, L, w), np.float32) for k, w in (("ak", 1024), ("av", 1024), ("bk", 1024), ("bv", 1024), ("lf", 8))}
    sk = {k: np.zeros((1, dec_batch, 64, w), np.float32) for k, w in (("ak", 1024), ("av", 1024), ("bk", 1024), ("bv", 1024), ("lf", 8))}
    for core in range(8):
        s, p = core // 2, core % 2
        r = results[core]
        for i in range(NP):
            tb = 2 * i + p
            y_p[s, tb * 512:(tb + 1) * 512] = r["y_out"][i * 512:(i + 1) * 512]
            for k in pk:
                pk[k][0, s, 16 + tb * 512:16 + (tb + 1) * 512] = r[k + "_out"][i * 512:(i + 1) * 512]
        m0 = NP * 512
        y_s[4 * core:4 * core + 4] = r["y_out"][m0:m0 + 256].reshape(4, 64, D)
        for k in sk:
            sk[k][0, 4 * core:4 * core + 4] = r[k + "_out"][m0:m0 + 256].reshape(4, 64, -1)
        if p == 0:
            for k in pk:
                pk[k][0, s, 0:16] = r[k + "_out"][m0 + 256:m0 + 272]
    return (y_p, y_s,
            pk["ak"].reshape(1, batch, L, 4, 2, 128), pk["av"].reshape(1, batch, L, 4, 256),
            pk["bk"].reshape(1, batch, L, 8, 128), pk["bv"].reshape(1, batch, L, 8, 128), pk["lf"],
            sk["ak"].reshape(1, dec_batch, 64, 4, 2, 128), sk["av"].reshape(1, dec_batch, 64, 4, 256),
            sk["bk"].reshape(1, dec_batch, 64, 8, 128), sk["bv"].reshape(1, dec_batch, 64, 8, 128), sk["lf"])


def run(cfg, inp, phases=3, trace=False):
    nc = build(cfg, phases)
    maps = make_in_maps(cfg, inp)
    res = run_bass_kernel_spmd(nc, maps, core_ids=list(range(8)), trace=trace)
    return assemble(cfg, res.results), res


def kernel(**inputs):
    cfg = Cfg()
    out, _ = run(cfg, inputs)
    return out
```
